# Optimizing a Trainium2 kernel written in Bass

```python
import math
import jax, jax.numpy as jnp
from jax import lax
import numpy as np

D_MODEL = 1024
BATCH = 8
SEQ = 2048
DEPTH = 1
DEC_BATCH = 128
DEC_SEQ = 8
PAST_LEN = 16384
PAGE_SIZE = 128

MIX_WIDTH = D_MODEL
S5_WIDTH = MIX_WIDTH // 2
S5_GROUP = 16
S5_GROUPS = S5_WIDTH // S5_GROUP
S5_STATE = 64
HG_WIDTH = MIX_WIDTH - S5_WIDTH
HG_HEAD_DIM = 128
HG_HEADS = HG_WIDTH // HG_HEAD_DIM
HG_CHUNK = 64
IN_COLS = S5_WIDTH + 4 * HG_WIDTH
D_FF = ((-(-8 * D_MODEL // 3) + 255) // 256) * 256
EPS = 1e-6
DT_MIN = 1e-3
DT_MAX = 1e-1

kernel_name = "hymba_s5_hgrn2_decode_step"


def rmsnorm(x, g):
    xf = x.astype(jnp.float32)
    r = xf * lax.rsqrt(jnp.mean(xf * xf, axis=-1, keepdims=True) + EPS)
    return (r * g.astype(jnp.float32)).astype(x.dtype)


def s5_discretize(a_re, a_im, log_dt, b_re, b_im):
    f32 = jnp.float32
    a_re = a_re.astype(f32); a_im = a_im.astype(f32)
    dt = jnp.exp(log_dt.astype(f32))[:, None]
    mag = jnp.exp(a_re * dt)
    ab_re = mag * jnp.cos(a_im * dt)
    ab_im = mag * jnp.sin(a_im * dt)
    den = a_re * a_re + a_im * a_im
    nr = ab_re - 1.0
    ni = ab_im
    f_re = (nr * a_re + ni * a_im) / den
    f_im = (ni * a_re - nr * a_im) / den
    b_re = b_re.astype(f32); b_im = b_im.astype(f32)
    bb_re = f_re[..., None] * b_re - f_im[..., None] * b_im
    bb_im = f_re[..., None] * b_im + f_im[..., None] * b_re
    return ab_re, ab_im, bb_re, bb_im


def s5_combine(e1, e2):
    a1r, a1i, b1r, b1i = e1
    a2r, a2i, b2r, b2i = e2
    return (a2r * a1r - a2i * a1i,
            a2r * a1i + a2i * a1r,
            a2r * b1r - a2i * b1i + b2r,
            a2r * b1i + a2i * b1r + b2i)


def s5_mixer(u, h0_re, h0_im, a_re, a_im, log_dt, b_re, b_im, c_re, c_im, d_skip, w_glu):
    f32 = jnp.float32
    B_, T, _ = u.shape
    ab_re, ab_im, bb_re, bb_im = s5_discretize(a_re, a_im, log_dt, b_re, b_im)
    uf = u.astype(f32)
    ug = uf.reshape(B_, T, S5_GROUPS, S5_GROUP)
    bu_re = jnp.einsum('btgc,gnc->btgn', ug, bb_re)
    bu_im = jnp.einsum('btgc,gnc->btgn', ug, bb_im)
    h0_re = h0_re.astype(f32); h0_im = h0_im.astype(f32)
    bu_re = bu_re.at[:, 0].add(ab_re * h0_re - ab_im * h0_im)
    bu_im = bu_im.at[:, 0].add(ab_re * h0_im + ab_im * h0_re)
    a_r = jnp.broadcast_to(ab_re, bu_re.shape)
    a_i = jnp.broadcast_to(ab_im, bu_im.shape)
    _, _, h_re, h_im = lax.associative_scan(s5_combine, (a_r, a_i, bu_re, bu_im), axis=1)
    y = (jnp.einsum('btgn,gcn->btgc', h_re, c_re.astype(f32))
         - jnp.einsum('btgn,gcn->btgc', h_im, c_im.astype(f32)))
    y = y.reshape(B_, T, S5_WIDTH) + d_skip.astype(f32) * uf
    g = jax.nn.gelu(y)
    out = g * jax.nn.sigmoid(g @ w_glu.astype(f32))
    return out.astype(u.dtype), h_re[:, -1], h_im[:, -1]


def hgrn2_mixer(q, fz, iv, og, lb, s0, g_norm):
    f32 = jnp.float32
    B_, T, _ = q.shape
    f = lb + (1.0 - lb) * jax.nn.sigmoid(fz.astype(f32))
    log_f = jnp.log(f)
    k = 1.0 - f
    chunk = math.gcd(HG_CHUNK, T)
    n_chunks = T // chunk

    def to_chunks(t):
        t = t.astype(f32).reshape(B_, n_chunks, chunk, HG_HEADS, HG_HEAD_DIM)
        return t.transpose(1, 0, 3, 2, 4)

    qc, kc, vc, lc = (to_chunks(t) for t in (q, k, iv, log_f))
    causal = jnp.tril(jnp.ones((chunk, chunk), dtype=bool))[:, :, None]

    def step(S, xs):
        qb, kb, vb, lfb = xs
        G = jnp.cumsum(lfb, axis=-2)
        diff = G[..., :, None, :] - G[..., None, :, :]
        decay = jnp.where(causal, jnp.exp(jnp.where(causal, diff, 0.0)), 0.0)
        att = jnp.einsum('bhtk,bhsk,bhtsk->bhts', qb, kb, decay)
        o = (jnp.einsum('bhts,bhsv->bhtv', att, vb)
             + jnp.einsum('bhtk,bhkv->bhtv', qb * jnp.exp(G), S))
        g_last = G[..., -1:, :]
        S_new = (jnp.exp(g_last[..., 0, :])[..., None] * S
                 + jnp.einsum('bhsk,bhsv->bhkv', kb * jnp.exp(g_last - G), vb))
        return S_new, o

    S_last, o = lax.scan(step, s0.astype(f32), (qc, kc, vc, lc))
    o = o.transpose(1, 0, 3, 2, 4).reshape(B_, T, HG_HEADS, HG_HEAD_DIM)
    o = o * lax.rsqrt(jnp.mean(o * o, axis=-1, keepdims=True) + EPS) * g_norm.astype(f32)
    gate = jax.nn.silu(og.astype(f32)).reshape(B_, T, HG_HEADS, HG_HEAD_DIM)
    o = (o * gate).reshape(B_, T, HG_WIDTH)
    return o.astype(q.dtype), S_last


def trunk(x, st_re, st_im, st_hg, lb_param, norm_mix, w_in, s5_a_re, s5_a_im, s5_log_dt,
          s5_b_re, s5_b_im, s5_c_re, s5_c_im, s5_d, s5_w_glu, hg_norm, w_out,
          norm_ffn, w_gate, w_up, w_down, norm_final):
    lb_all = jnp.cumsum(jax.nn.softmax(lb_param.astype(jnp.float32), axis=0), axis=0)
    new_re, new_im, new_hg = [], [], []
    for l in range(DEPTH):
        h = rmsnorm(x, norm_mix[l])
        proj = h @ w_in[l]
        u = proj[..., :S5_WIDTH]
        q = proj[..., S5_WIDTH:S5_WIDTH + HG_WIDTH]
        fz = proj[..., S5_WIDTH + HG_WIDTH:S5_WIDTH + 2 * HG_WIDTH]
        iv = proj[..., S5_WIDTH + 2 * HG_WIDTH:S5_WIDTH + 3 * HG_WIDTH]
        og = proj[..., S5_WIDTH + 3 * HG_WIDTH:]
        y5, h_re, h_im = s5_mixer(u, st_re[l], st_im[l], s5_a_re[l], s5_a_im[l], s5_log_dt[l],
                                  s5_b_re[l], s5_b_im[l], s5_c_re[l], s5_c_im[l], s5_d[l], s5_w_glu[l])
        yh, S_hg = hgrn2_mixer(q, fz, iv, og, lb_all[l], st_hg[l], hg_norm[l])
        x = x + jnp.concatenate([y5, yh], axis=-1) @ w_out[l]
        h2 = rmsnorm(x, norm_ffn[l])
        x = x + (jax.nn.silu(h2 @ w_gate[l]) * (h2 @ w_up[l])) @ w_down[l]
        new_re.append(h_re); new_im.append(h_im); new_hg.append(S_hg)
    y = rmsnorm(x, norm_final)
    return y, jnp.stack(new_re), jnp.stack(new_im), jnp.stack(new_hg)


def setup_inputs(seed: int = 0) -> dict:
    key = jax.random.key(seed)
    ks = jax.random.split(key, 32)
    f32 = jnp.float32
    nrm = lambda k, s, sc: jax.random.normal(k, s, f32) * sc
    n_idx = jnp.arange(S5_STATE, dtype=f32)
    a_re = -0.5 * jnp.exp(nrm(ks[5], (DEPTH, S5_GROUPS, S5_STATE), 0.02))
    a_im = jnp.pi * n_idx + nrm(ks[6], (DEPTH, S5_GROUPS, S5_STATE), 0.01)
    log_dt = jax.random.uniform(ks[7], (DEPTH, S5_GROUPS), f32, math.log(DT_MIN), math.log(DT_MAX))
    return {
        "x_prompt": nrm(ks[0], (BATCH, SEQ, D_MODEL), 1.0),
        "x_sample": nrm(ks[1], (DEC_BATCH, DEC_SEQ, D_MODEL), 1.0),
        "state_s5_re": nrm(ks[2], (DEPTH, DEC_BATCH, S5_GROUPS, S5_STATE), 0.5),
        "state_s5_im": nrm(ks[3], (DEPTH, DEC_BATCH, S5_GROUPS, S5_STATE), 0.5),
        "state_hgrn": nrm(ks[4], (DEPTH, DEC_BATCH, HG_HEADS, HG_HEAD_DIM, HG_HEAD_DIM), 0.5),
        "lb_param": nrm(ks[8], (DEPTH + 1, HG_WIDTH), 0.5),
        "norm_mix": 1.0 + nrm(ks[9], (DEPTH, D_MODEL), 0.02),
        "w_in": nrm(ks[10], (DEPTH, D_MODEL, IN_COLS), D_MODEL ** -0.5),
        "s5_a_re": a_re,
        "s5_a_im": a_im,
        "s5_log_dt": log_dt,
        "s5_b_re": nrm(ks[11], (DEPTH, S5_GROUPS, S5_STATE, S5_GROUP), (2 * S5_GROUP) ** -0.5),
        "s5_b_im": nrm(ks[12], (DEPTH, S5_GROUPS, S5_STATE, S5_GROUP), (2 * S5_GROUP) ** -0.5),
        "s5_c_re": nrm(ks[13], (DEPTH, S5_GROUPS, S5_GROUP, S5_STATE), (2 * S5_STATE) ** -0.5),
        "s5_c_im": nrm(ks[14], (DEPTH, S5_GROUPS, S5_GROUP, S5_STATE), (2 * S5_STATE) ** -0.5),
        "s5_d": nrm(ks[15], (DEPTH, S5_WIDTH), 1.0),
        "s5_w_glu": nrm(ks[16], (DEPTH, S5_WIDTH, S5_WIDTH), S5_WIDTH ** -0.5),
        "hg_norm": 1.0 + nrm(ks[17], (DEPTH, HG_HEAD_DIM), 0.02),
        "w_out": nrm(ks[18], (DEPTH, MIX_WIDTH, D_MODEL), MIX_WIDTH ** -0.5),
        "norm_ffn": 1.0 + nrm(ks[19], (DEPTH, D_MODEL), 0.02),
        "w_gate": nrm(ks[20], (DEPTH, D_MODEL, D_FF), D_MODEL ** -0.5),
        "w_up": nrm(ks[21], (DEPTH, D_MODEL, D_FF), D_MODEL ** -0.5),
        "w_down": nrm(ks[22], (DEPTH, D_FF, D_MODEL), D_FF ** -0.5),
        "norm_final": 1.0 + nrm(ks[23], (D_MODEL,), 0.02),
    }


def reference(x_prompt, x_sample, state_s5_re, state_s5_im, state_hgrn, lb_param, norm_mix, w_in,
              s5_a_re, s5_a_im, s5_log_dt, s5_b_re, s5_b_im, s5_c_re, s5_c_im, s5_d, s5_w_glu,
              hg_norm, w_out, norm_ffn, w_gate, w_up, w_down, norm_final):
    f32 = jnp.float32
    bp = x_prompt.shape[0]
    zero_re = jnp.zeros((DEPTH, bp, S5_GROUPS, S5_STATE), f32)
    zero_hg = jnp.zeros((DEPTH, bp, HG_HEADS, HG_HEAD_DIM, HG_HEAD_DIM), f32)
    y_prompt, p_re, p_im, p_hg = trunk(
        x_prompt, zero_re, zero_re, zero_hg, lb_param, norm_mix, w_in, s5_a_re, s5_a_im, s5_log_dt,
        s5_b_re, s5_b_im, s5_c_re, s5_c_im, s5_d, s5_w_glu, hg_norm, w_out,
        norm_ffn, w_gate, w_up, w_down, norm_final)
    y_sample, s_re, s_im, s_hg = trunk(
        x_sample, state_s5_re, state_s5_im, state_hgrn, lb_param, norm_mix, w_in, s5_a_re, s5_a_im,
        s5_log_dt, s5_b_re, s5_b_im, s5_c_re, s5_c_im, s5_d, s5_w_glu, hg_norm, w_out,
        norm_ffn, w_gate, w_up, w_down, norm_final)
    return (y_prompt, y_sample, p_re, p_im, p_hg, s_re, s_im, s_hg)
```

```python
import math
import numpy as np
import ml_dtypes
from contextlib import ExitStack
import concourse.bass as bass
import concourse.mybir as mybir
from concourse.bass_utils import run_bass_kernel_spmd

F32 = mybir.dt.float32
BF16 = mybir.dt.bfloat16
AF = mybir.ActivationFunctionType
ALU = mybir.AluOpType
PE, ACT, DVE, POOL, SP = "tensor", "scalar", "vector", "gpsimd", "sync"
ENGS = [PE, ACT, DVE, POOL, SP]
NT, NTP, NTILE, NB = 2176, 2048, 17, 272
CHUNKS = [(0, 512), (512, 512), (1024, 512), (1536, 512), (2048, 128)]
TWO_PI = 2.0 * math.pi
MAGIC = 12582912.0
PI_SAFE = 3.14159


class Res:
    __slots__ = ("name", "w", "r", "dsem", "dcnt", "ws")

    def __init__(self, name=""):
        self.name = name
        self.w = None
        self.ws = []
        self.r = {}
        self.dsem = None
        self.dcnt = 0


class Op:
    __slots__ = ("eng", "fn", "deps", "dma_res", "dma_val", "milestone", "val", "ddeps")

    def __init__(self, eng, fn):
        self.eng = eng
        self.fn = fn
        self.deps = []
        self.ddeps = []
        self.dma_res = None
        self.dma_val = 0
        self.milestone = False
        self.val = 0


import types


def _freeze(fn):
    if fn.__closure__ is None:
        return fn
    cells = []
    for c in fn.__closure__:
        try:
            cells.append(types.CellType(c.cell_contents))
        except ValueError:
            cells.append(c)
    return types.FunctionType(fn.__code__, fn.__globals__, fn.__name__, fn.__defaults__, tuple(cells))


class Sched:
    def __init__(self, nc):
        self.nc = nc
        self.streams = {e: [] for e in ENGS}
        self.dma_resources = []
        self.pending = {e: [] for e in ENGS}
        self.last_dma = {}

    def barrier(self, extra_res=()):
        lasts = [self.streams[e][-1] for e in ENGS if self.streams[e]]
        lasts += [self.last_dma[id(r)] for r in extra_res if id(r) in self.last_dma]
        for e in ENGS:
            self.pending[e] = list(lasts)

    def _dep_on(self, op, prev, same_ok):
        if prev is None:
            return
        if prev.dma_res is not None:
            op.ddeps.append((prev.dma_res, prev.dma_val))
            return
        if prev.eng == op.eng and not same_ok:
            return
        op.deps.append(prev)
        prev.milestone = True

    def op(self, eng, fn, reads=(), writes=(), dma=None, uwrites=()):
        o = Op(eng, _freeze(fn))
        is_dma = dma is not None
        if self.pending[eng]:
            for p in self.pending[eng]:
                self._dep_on(o, p, same_ok=(eng != PE))
            self.pending[eng] = []
        for r in reads:
            self._dep_on(o, r.w, same_ok=(is_dma or eng != PE))
            for w_ in r.ws:
                self._dep_on(o, w_, same_ok=(is_dma or eng != PE))
        for r in writes:
            self._dep_on(o, r.w, same_ok=(is_dma or eng != PE))
            for w_ in r.ws:
                self._dep_on(o, w_, same_ok=(is_dma or eng != PE))
            for e, rd in r.r.items():
                self._dep_on(o, rd, same_ok=(is_dma or eng != PE))
        for r in uwrites:
            self._dep_on(o, r.w, same_ok=(is_dma or eng != PE))
            for e, rd in r.r.items():
                self._dep_on(o, rd, same_ok=(is_dma or eng != PE))
        if is_dma:
            if dma.dsem is None:
                self.dma_resources.append(dma)
                dma.dsem = True
            dma.dcnt += 1
            o.dma_res = dma
            o.dma_val = dma.dcnt * 16
            self.last_dma[id(dma)] = o
        for r in reads:
            key = eng if not is_dma else ("dma", id(o))
            r.r[key] = o
        for r in writes:
            r.w = o
            r.ws = []
            r.r = {}
        for r in uwrites:
            r.ws.append(o)
        self.streams[eng].append(o)
        return o

    def emit(self, final_waits=()):
        nc = self.nc
        with ExitStack() as es:
            sems = {e: es.enter_context(nc.semaphore("s_" + e)) for e in ENGS}
            for i, r in enumerate(self.dma_resources):
                r.dsem = es.enter_context(nc.semaphore("d%d" % i))
            for e in ENGS:
                c = 0
                for o in self.streams[e]:
                    if o.dma_res is None and o.milestone:
                        c += 1
                        o.val = c
            block = es.enter_context(nc.Block())

            def run(ename):
                def body(engine):
                    known = {}
                    dknown = {}
                    for o in self.streams[ename]:
                        for d in o.deps:
                            if known.get(d.eng, 0) < d.val:
                                engine.wait_ge(sems[d.eng], d.val)
                                known[d.eng] = d.val
                        for (r, v) in o.ddeps:
                            if dknown.get(id(r), 0) < v:
                                engine.wait_ge(r.dsem, v)
                                dknown[id(r)] = v
                        ins = o.fn(engine)
                        if o.dma_res is not None:
                            ins.then_inc(o.dma_res.dsem, 16)
                        elif o.milestone:
                            ins.then_inc(sems[ename], 1)
                    if ename == SP:
                        for r in final_waits:
                            engine.wait_ge(r.dsem, r.dcnt * 16)
                return body

            block.tensor(run(PE))
            block.scalar(run(ACT))
            block.vector(run(DVE))
            block.gpsimd(run(POOL))
            block.sync(run(SP))


class RD(dict):
    def __missing__(self, k):
        v = Res(str(k))
        self[k] = v
        return v


class _Stop(Exception):
    pass


STOP = None
SKIPB = None


def build_nc():
    nc = bass.Bass("TRN2", target_bir_lowering=False)
    S = Sched(nc)
    R = RD()
    out_res = []
    try:
        _build_body(nc, S, R, out_res)
    except _Stop:
        pass
    S.emit(final_waits=list({id(r): r for r in out_res}.values()))
    return nc


def _build_body(nc, S, R, out_res):
    def stop_at(tag):
        if STOP == tag:
            raise _Stop()

    def din(name, shape, dt=F32):
        return nc.dram_tensor(name, list(shape), dt, kind="ExternalInput").ap()

    def dout(name, shape):
        return nc.dram_tensor(name, list(shape), F32, kind="ExternalOutput").ap()

    xin = din("xin", [NT, 1024])
    s5re_in = din("s5re_in", [128, 16, 16])
    s5im_in = din("s5im_in", [128, 16, 16])
    hg_in = din("hg_in", [16, 4, 128, 128])
    w_in = din("w_in", [1024, 2560])
    w_glu = din("w_glu", [512, 512])
    w_out = din("w_out", [1024, 1024])
    w_gate = din("w_gate", [1024, 2816])
    w_up = din("w_up", [1024, 2816])
    w_down = din("w_down", [2816, 1024])
    gmix_d = din("gmix", [128, 8])
    gffn_d = din("gffn", [128, 8])
    nfin_d = din("nfin", [1024])
    hgn_d = din("hgn", [128, 1])
    lbp_d = din("lbp", [128, 8])
    are_d = din("a_re", [128, 16])
    aim_d = din("a_im", [128, 16])
    ldt_d = din("ldt", [128, 16])
    bre_d = din("b_re", [128, 16, 16])
    bim_d = din("b_im", [128, 16, 16])
    cre_d = din("c_re", [128, 16, 16])
    cim_d = din("c_im", [128, 16, 16])
    dsk_d = din("dsk", [128, 4])
    ident_d = din("ident", [128, 128], BF16)
    m64_d = din("m64", [128, 128], BF16)
    m8_d = din("m8", [128, 128], BF16)
    rmask_d = din("rmask", [128, NT])
    rowm_d = din("rowm", [128, 16])
    jrow_d = din("jrow", [128, 256])

    y_o = dout("y", [NT, 1024])
    s5p_re_o = dout("s5p_re", [128, 16])
    s5p_im_o = dout("s5p_im", [128, 16])
    hgp_o = dout("hgp", [4, 128, 128])
    s5s_re_o = dout("s5s_re", [128, 16, 16])
    s5s_im_o = dout("s5s_im", [128, 16, 16])
    hgs_o = dout("hgs", [16, 4, 128, 128])

    LIMIT = 229376
    cur = [16640]
    allocs = {}

    def sb(name, shape, dt, at=None):
        nbytes = int(np.prod(shape[1:])) * (4 if dt == F32 else 2)
        nbytes = (nbytes + 63) // 64 * 64
        if at is None:
            o = cur[0]
            cur[0] += nbytes
        else:
            o = at
        assert o + nbytes <= LIMIT, (name, o, nbytes)
        allocs[name] = (o, nbytes)
        return nc.alloc_sbuf_tensor_at(name, list(shape), dt, offset=o)

    NBANK = 8
    banks = [nc.alloc_psum_tensor("bank%d" % i, [128, 512], F32) for i in range(NBANK)]
    bank_i = [0]

    def nextbank():
        i = bank_i[0] % NBANK
        bank_i[0] += 1
        return banks[i], R[("bank", i)]

    pool_i = {}

    def poolbank(lo, n):
        k = pool_i.get((lo, n), 0)
        pool_i[(lo, n)] = k + 1
        i = lo + k % n
        return banks[i], R[("bank", i)]

    def fixbank(i):
        return banks[i], R[("bank", i)]

    alt = [0]

    def alt_eng():
        alt[0] += 1
        return ACT if alt[0] % 2 else DVE

    def copy_op(eng, out, in_, reads, writes):
        if eng == ACT:
            S.op(ACT, lambda e: e.activation(out=out, in_=in_, func=AF.Copy), reads=reads, writes=writes)
        else:
            S.op(eng, lambda e: e.tensor_copy(out=out, in_=in_), reads=reads, writes=writes)

    def load(out, in_, res, eng=SP):
        S.op(eng, lambda e: e.dma_start(out=out, in_=in_), writes=[res], dma=res)

    def store(out, in_, res):
        S.op(SP, lambda e: e.dma_start(out=out, in_=in_), reads=[res], dma=res)
        out_res.append(res)

    ident = sb("ident", [128, 128], BF16)
    m64 = sb("m64", [128, 128], BF16)
    m8 = sb("m8", [128, 128], BF16)
    onesm = sb("onesm", [128, 128], BF16)
    rmask = sb("rmask", [128, NT], F32)
    rowm = sb("rowm", [128, 16], F32)
    gmix = sb("gmix", [128, 8], F32)
    gffn = sb("gffn", [128, 8], F32)
    nfin = sb("nfin", [128, 1024], F32)
    hgn = sb("hgn", [128, 1], F32)
    lbp = sb("lbp", [128, 8], F32)
    lb = sb("lb", [128, 4], F32)
    oml = sb("oml", [128, 4], F32)
    noml = sb("noml", [128, 4], F32)
    dsk = sb("dsk", [128, 4], F32)
    epsb = sb("epsb", [128, 1], F32)
    ssb = sb("ssb", [128, 4], F32)
    Cst = R["const"]
    for t, d in ((ident, ident_d), (m64, m64_d), (m8, m8_d), (rmask, rmask_d), (rowm, rowm_d), (gmix, gmix_d),
                 (gffn, gffn_d), (hgn, hgn_d), (lbp, lbp_d), (dsk, dsk_d)):
        load(t[:], d, Cst)
    load(nfin[:], nfin_d.partition_broadcast(128), Cst)
    S.op(POOL, lambda e: e.memset(epsb[:], 1e-6), writes=[Cst])
    S.op(POOL, lambda e: e.memset(onesm[:], 1.0 / 128), writes=[Cst])
    S.op(DVE, lambda e: e.tensor_tensor(out=lb[:], in0=lbp[:, 0:4], in1=lbp[:, 4:8], op=ALU.subtract), reads=[Cst], writes=[R["lb"]])
    S.op(ACT, lambda e: e.activation(out=lb[:], in_=lb[:], func=AF.Sigmoid), reads=[R["lb"]], writes=[R["lb"]])
    S.op(DVE, lambda e: e.tensor_scalar(out=oml[:], in0=lb[:], scalar1=-1.0, scalar2=1.0, op0=ALU.mult, op1=ALU.add), reads=[R["lb"]], writes=[R["oml"]])
    S.op(DVE, lambda e: e.tensor_scalar(out=noml[:], in0=oml[:], scalar1=-1.0, scalar2=None, op0=ALU.mult), reads=[R["oml"]], writes=[R["noml"]])

    mixT = sb("mixT", [128, 8, NT], BF16)
    uP = sb("uP", [128, 4, NT], BF16)
    mark_L1 = cur[0]
    qT = sb("qT", [128, 4, NT], BF16)
    kk = sb("kk", [128, 4, NT], BF16)
    sog = sb("sog", [128, 4, NT], BF16)
    iv = sb("iv", [128, NTILE, 512], BF16)
    logf = sb("logf", [128, 4, NT], F32)
    mark_B = cur[0]
    hT = sb("hT", [128, 8, NT], BF16, at=allocs["mixT"][0])
    xbuf = [sb("xbuf%d" % i, [128, 1024], F32) for i in range(4)]
    xnb = [sb("xnb%d" % i, [128, 1024], BF16) for i in range(4)]
    NXB = [4]
    wct = [sb("wct%d" % i, [128, 8, 128], BF16) for i in range(2)]
    wiv = sb("wiv", [128, 8, 512], BF16)
    sigt = [sb("sigt%d" % i, [128, 512], F32) for i in range(2)]

    def tile_of(c0, n):
        return [R[("tok", t)] for t in range(c0 // 128, (c0 + n) // 128)]

    def norm_stats(tile, src_ap, src_res):
        b = tile % NXB[0]
        junk = xnb[b]
        ss = ssb[:, b:b + 1]
        rs = R[("ss", b)]
        rxn = R[("xn", b)]
        S.op(DVE, lambda e: e.memset(ss, 0.0), writes=[rs])
        S.op(ACT, lambda e: e.activation(out=junk[:], in_=src_ap, func=AF.Square, accum_out=ss), reads=[src_res], writes=[rxn, rs])
        S.op(ACT, lambda e: e.activation(out=ss, in_=ss, func=AF.Sqrt, scale=1.0 / 1024, bias=epsb[:, 0:1]), reads=[rs, Cst], writes=[rs])
        S.op(DVE, lambda e: e.reciprocal(out=ss, in_=ss), reads=[rs], writes=[rs])
        S.op(DVE, lambda e: e.tensor_scalar(out=junk[:], in0=src_ap, scalar1=ss, scalar2=None, op0=ALU.mult), reads=[rs, src_res], writes=[rxn])

    def norm_trans(tile, gvec, dstT, dst_key):
        b = tile % NXB[0]
        junk = xnb[b]
        rxn = R[("xn", b)]
        bk, rb = nextbank()
        pst = bk[:].bitcast(BF16)
        for c in range(8):
            S.op(PE, lambda e, c=c: e.transpose(out=pst[:, c * 128:(c + 1) * 128], in_=junk[:, c * 128:(c + 1) * 128], identity=ident[:]),
                 reads=[rxn, Cst], writes=[rb])
        o = dstT[:, :, tile * 128:(tile + 1) * 128]
        i_ = pst[:, 0:1024].rearrange("p (c n) -> p c n", c=8)
        gb = gvec[:, :].unsqueeze(2).to_broadcast([128, 8, 128])
        S.op(DVE, lambda e: e.tensor_tensor(out=o, in0=i_, in1=gb, op=ALU.mult), reads=[rb, Cst], writes=[R[(dst_key, tile)]])

    def preA(t):
        b = t % 4
        load(xbuf[b][:], xin[t * 128:(t + 1) * 128, :], R[("xbuf", b)])
        norm_stats(t, xbuf[b][:], R[("xbuf", b)])

    for t in range(3):
        preA(t)
    for t in range(NTILE):
        if t + 3 < NTILE:
            preA(t + 3)
        norm_trans(t, gmix, hT, "hT")

    stop_at("A")
    def hT_res(c0, n):
        return [R[("hT", t)] for t in range(c0 // 128, (c0 + n) // 128)]

    wi = [0]
    for ct in list(range(0, 12)) + list(range(16, 20)):
        wb = wi[0] % 2
        wi[0] += 1
        load(wct[wb][:], w_in[:, ct * 128:(ct + 1) * 128].rearrange("(c p) n -> p c n", p=128), R[("wct", wb)], eng=POOL)
        kind, h = ct // 4, ct % 4
        for (c0, n) in CHUNKS:
            bk, rb = nextbank()
            for c in range(8):
                S.op(PE, lambda e, c=c, bk=bk, wb=wb, c0=c0, n=n: e.matmul(bk[:, 0:n], lhsT=wct[wb][:, c, :], rhs=hT[:, c, c0:c0 + n], start=(c == 0), stop=(c == 7)),
                     reads=[R[("wct", wb)]] + hT_res(c0, n), writes=[rb])
            if kind == 0:
                j0, nbk = c0 // 8, n // 8
                o = uP[:, h, :].rearrange("p (i j) -> p i j", i=8)[:, :, j0:j0 + nbk]
                i_ = bk[:, 0:n].rearrange("p (j i) -> p i j", i=8)
                copy_op(DVE, o, i_, [rb], [R[("uP", h)]])
            elif kind == 1:
                copy_op(ACT, qT[:, h, c0:c0 + n], bk[:, 0:n], [rb], [R[("qT", h)]])
            elif kind == 2:
                S.op(ACT, lambda e, bk=bk, n=n, h=h, c0=c0: e.activation(out=logf[:, h, c0:c0 + n], in_=bk[:, 0:n], func=AF.Sigmoid), reads=[rb], writes=[R[("logf", h)]])
                if c0 == 2048:
                    S.op(DVE, lambda e, h=h: e.tensor_scalar(out=kk[:, h, :], in0=logf[:, h, :], scalar1=noml[:, h:h + 1], scalar2=oml[:, h:h + 1], op0=ALU.mult, op1=ALU.add),
                         reads=[R[("logf", h)], R["oml"], R["noml"]], writes=[R[("kk", h)]])
                    S.op(ACT, lambda e, h=h: e.activation(out=logf[:, h, :], in_=logf[:, h, :], func=AF.Ln, scale=oml[:, h:h + 1], bias=lb[:, h:h + 1]),
                         reads=[R[("logf", h)], R["oml"], R["lb"]], writes=[R[("logf", h)]])
            else:
                S.op(ACT, lambda e, bk=bk, n=n, h=h, c0=c0: e.activation(out=sog[:, h, c0:c0 + n], in_=bk[:, 0:n], func=AF.Silu), reads=[rb], writes=[R[("sog", h)]])
                S.op(DVE, lambda e, n=n, h=h, c0=c0: e.tensor_scalar(out=sog[:, h, c0:c0 + n], in0=sog[:, h, c0:c0 + n], scalar1=hgn[:, 0:1], scalar2=None, op0=ALU.mult), reads=[R[("sog", h)], Cst], writes=[R[("sog", h)]])
    load(wiv[:], w_in[:, 1536:2048].rearrange("(c p) n -> p c n", p=128), R["wiv"], eng=POOL)
    for t in range(NTILE):
        bk, rb = nextbank()
        for c in range(8):
            S.op(PE, lambda e, c=c, bk=bk, t=t: e.matmul(bk[:, :], lhsT=hT[:, c, t * 128:(t + 1) * 128], rhs=wiv[:, c, :], start=(c == 0), stop=(c == 7)),
                 reads=[R["wiv"], R[("hT", t)]], writes=[rb])
        copy_op(alt_eng(), iv[:, t, :], bk[:, :], [rb], [R[("iv", t)]])

    stop_at("B")
    cur[0] = mark_B
    BAR = R["barrier_B"]

    def barrier(tag, extra=()):
        S.barrier(extra)

    barrier("B", [R[("wct", 0)], R[("wct", 1)], R["wiv"], R[("xbuf", 0)], R[("xbuf", 1)], R[("xbuf", 2)], R[("xbuf", 3)], Cst])

    stop_at("D0")
    Gt = sb("Gt", [128, NT], F32)
    eG = Gt
    eGn = sb("eGn", [128, NT], F32)
    eGl = sb("eGl", [128, 4, 48], F32)
    kgTM = sb("kgTM", [128, NTILE, 512], BF16, at=allocs["mixT"][0])
    attm = [sb("attm%d" % i, [128, 128], BF16) for i in range(4)]
    Sf = [sb("Sf%d" % i, [128, 128], F32) for i in range(4)]
    Sb_ = [sb("Sb%d" % i, [128, 128], BF16) for i in range(4)]
    stmp = [sb("stmp%d" % i, [128, 128], F32) for i in range(4)]
    sqb = [sb("sqb%d" % i, [128, 128], BF16) for i in range(4)]
    osb = [sb("osb%d" % i, [128, 128], F32) for i in range(4)]
    rstd = [sb("rstd%d" % i, [128, 128], F32) for i in range(2)]
    _save = cur[0]
    cur[0] = allocs["Gt"][0]
    sinf = [sb("sinf%d" % i, [128, 4, 128], F32) for i in range(4)]
    sinb = [sb("sinb%d" % i, [128, 4, 128], BF16) for i in range(2)]
    kgm = [sb("kgm%d" % i, [128, 512], BF16) for i in range(2)]
    sout = [sb("sout%d" % i, [128, 4, 128], F32) for i in range(2)]
    assert cur[0] <= allocs["eGn"][0] + allocs["eGn"][1]
    cur[0] = _save

    for h in range(4):
        rG = R["Gt"]
        S.op(DVE, lambda e, h=h: e.tensor_tensor_scan(out=Gt[:], data0=rmask[:], data1=logf[:, h, :], initial=0.0, op0=ALU.mult, op1=ALU.add),
             reads=[Cst, R[("logf", h)]], writes=[rG, R["eG"]])
        stop_at("D1")
        S.op(ACT, lambda e: e.activation(out=eGn[:], in_=Gt[:], func=AF.Exp, scale=-1.0), reads=[rG], writes=[R["eGn"]])
        S.op(ACT, lambda e: e.activation(out=eG[:], in_=Gt[:], func=AF.Exp), reads=[rG], writes=[rG, R["eG"]])
        stop_at("D2")
        S.op(DVE, lambda e, h=h: e.tensor_tensor(out=qT[:, h, :], in0=qT[:, h, :], in1=eG[:], op=ALU.mult), reads=[R["eG"], R[("qT", h)]], writes=[R[("qT", h)]])
        S.op(DVE, lambda e, h=h: e.tensor_tensor(out=kk[:, h, :], in0=kk[:, h, :], in1=eGn[:], op=ALU.mult), reads=[R["eGn"], R[("kk", h)]], writes=[R[("kk", h)]])
        stop_at("D3")
        S.op(DVE, lambda e, h=h: e.tensor_copy(out=eGl[:, h, 0:32], in_=eG[:, 63:2048:64]), reads=[R["eG"]], writes=[R["eGl"]])
        S.op(DVE, lambda e, h=h: e.tensor_copy(out=eGl[:, h, 32:48], in_=eG[:, 2055:NT:8]), reads=[R["eG"]], writes=[R["eGl"]])
        stop_at("D4")
        for t in range(NTILE):
            if t % 4 == 0:
                bk, rb = nextbank()
                pst = bk[:].bitcast(BF16)
            S.op(PE, lambda e, pst=pst, t=t, h=h: e.transpose(out=pst[:, (t % 4) * 128:(t % 4 + 1) * 128], in_=kk[:, h, t * 128:(t + 1) * 128], identity=ident[:]),
                 reads=[R[("kk", h)], Cst], writes=[rb])
            if t % 4 == 3 or t == NTILE - 1:
                t0 = t - (t % 4)
                nt_ = t - t0 + 1
                copy_op(alt_eng(), kgTM[:, t0:t0 + nt_, h * 128:(h + 1) * 128], pst[:, 0:nt_ * 128].rearrange("p (t k) -> p t k", k=128), [rb], [R["kgTM"]])

    stop_at("Dprep")
    for h in range(4):
        S.op(POOL, lambda e, h=h: e.memset(Sf[h][:], 0.0), writes=[R[("Sf", h)]])
        S.op(POOL, lambda e, h=h: e.memset(Sb_[h][:], 0.0), writes=[R[("Sb", h)]])

    def post1(t, h):
        bo, rbo = fixbank(h)
        b, b2 = h, h % 2
        S.op(DVE, lambda e: e.tensor_copy(out=osb[b][:], in_=bo[:, 0:128]), reads=[rbo], writes=[R[("osb", b)]])
        S.op(DVE, lambda e: e.tensor_tensor(out=sqb[b][:], in0=osb[b][:], in1=osb[b][:], op=ALU.mult), reads=[R[("osb", b)]], writes=[R[("sqb", b)]])

    def post2(t, h):
        b, b2 = h, h % 2
        bm, rbm = poolbank(6, 2)
        S.op(PE, lambda e: e.matmul(bm[:, 0:128], lhsT=onesm[:], rhs=sqb[b][:], start=True, stop=True), reads=[Cst, R[("sqb", b)]], writes=[rbm])
        S.op(ACT, lambda e: e.activation(out=rstd[b2][:], in_=bm[:, 0:128], func=AF.Ln, bias=epsb[:, 0:1]), reads=[rbm, Cst], writes=[R[("rstd", b2)]])
        S.op(ACT, lambda e: e.activation(out=rstd[b2][:], in_=rstd[b2][:], func=AF.Exp, scale=-0.5), reads=[R[("rstd", b2)]], writes=[R[("rstd", b2)]])
        S.op(POOL, lambda e: e.tensor_tensor(out=osb[b][:], in0=osb[b][:], in1=rstd[b2][:], op=ALU.mult), reads=[R[("rstd", b2)], R[("osb", b)]], writes=[R[("osb", b)]])
        S.op(POOL, lambda e: e.tensor_tensor(out=mixT[:, 4 + h, t * 128:(t + 1) * 128], in0=osb[b][:], in1=sog[:, h, t * 128:(t + 1) * 128], op=ALU.mult),
             reads=[R[("osb", b)], R[("sog", h)]], writes=[R[("mix", t)]])

    def hg_post(t, h, bo=None, rbo=None):
        post1(t, h)
        post2(t, h)

    def stageA(t, h, mask):
        cols = slice(t * 128, (t + 1) * 128)
        ba, rba = poolbank(4, 2)
        S.op(PE, lambda e: e.matmul(ba[:, 0:128], lhsT=kk[:, h, cols], rhs=qT[:, h, cols], start=True, stop=True),
             reads=[R[("kk", h)], R[("qT", h)]], writes=[rba])
        S.op(DVE, lambda e: e.tensor_tensor(out=attm[h][:], in0=ba[:, 0:128], in1=mask[:], op=ALU.mult), reads=[rba, Cst], writes=[R[("attm", h)]])

    def stageB(t, h):
        bo, rbo = fixbank(h)
        S.op(PE, lambda e: e.matmul(bo[:, 0:128], lhsT=iv[:, t, h * 128:(h + 1) * 128], rhs=attm[h][:], start=True, stop=False),
             reads=[R[("iv", t)], R[("attm", h)]], writes=[rbo])

    def stageC(t, c, h):
        ci = t * 2 + c
        bo, rbo = fixbank(h)
        ccols = slice(t * 128 + c * 64, t * 128 + c * 64 + 64)
        S.op(PE, lambda e: e.matmul(bo[:, c * 64:(c + 1) * 64], lhsT=Sb_[h][:], rhs=qT[:, h, ccols], start=False, stop=(c == 1)),
             reads=[R[("Sb", h)], R[("qT", h)]], writes=[rbo])
        bkv, rbkv = poolbank(6, 2)
        S.op(PE, lambda e: e.matmul(bkv[:, 0:128], lhsT=kgTM[c * 64:(c + 1) * 64, t, h * 128:(h + 1) * 128], rhs=iv[c * 64:(c + 1) * 64, t, h * 128:(h + 1) * 128], start=True, stop=True),
             reads=[R["kgTM"], R[("iv", t)]], writes=[rbkv])
        S.op(DVE, lambda e: e.tensor_tensor(out=stmp[h][:], in0=bkv[:, 0:128], in1=Sf[h][:], op=ALU.add), reads=[rbkv, R[("Sf", h)]], writes=[R[("stmp", h)]])
        S.op(DVE, lambda e: e.tensor_scalar(out=Sb_[h][:], in0=stmp[h][:], scalar1=eGl[:, h, ci:ci + 1], scalar2=None, op0=ALU.mult), reads=[R[("stmp", h)], R["eGl"]], writes=[R[("Sb", h)]])
        S.op(ACT, lambda e: e.activation(out=Sf[h][:], in_=stmp[h][:], func=AF.Copy, scale=eGl[:, h, ci:ci + 1]), reads=[R[("stmp", h)], R["eGl"]], writes=[R[("Sf", h)]])

    for h in range(4):
        stageA(0, h, m64)
    for t in range(16):
        if t > 0:
            for h in range(4):
                post1(t - 1, h)
        for h in range(4):
            stageB(t, h)
        for h in range(4):
            stageC(t, 0, h)
        if t > 0:
            for h in range(4):
                post2(t - 1, h)
        if t + 1 < 16:
            for h in range(4):
                stageA(t + 1, h, m64)
        for h in range(4):
            stageC(t, 1, h)
    for h in range(4):
        post1(15, h)
    for h in range(4):
        post2(15, h)
    for h in range(4):
        store(hgp_o[h], Sf[h][:], R[("Sf", h)])

    stop_at("Dp")
    t = 16
    cols = slice(2048, NT)
    bos = []
    for h in range(4):
        ba, rba = poolbank(4, 2)
        S.op(PE, lambda e, ba=ba, h=h: e.matmul(ba[:, 0:128], lhsT=kk[:, h, cols], rhs=qT[:, h, cols], start=True, stop=True), reads=[R[("kk", h)], R[("qT", h)]], writes=[rba])
        ab = h
        S.op(DVE, lambda e, ba=ba, ab=ab: e.tensor_tensor(out=attm[ab][:], in0=ba[:, 0:128], in1=m8[:], op=ALU.mult), reads=[rba, Cst], writes=[R[("attm", ab)]])
        bo, rbo = fixbank(h)
        bos.append((bo, rbo))
        S.op(PE, lambda e, bo=bo, ab=ab, h=h: e.matmul(bo[:, 0:128], lhsT=iv[:, 16, h * 128:(h + 1) * 128], rhs=attm[ab][:], start=True, stop=False),
             reads=[R[("iv", 16)], R[("attm", ab)]], writes=[rbo])
    ALIAS_D = [R["Gt"], R["eG"], R["eGn"]]

    def sload(q):
        b4 = q % 4
        S.op(SP, lambda e: e.dma_start(out=sinf[b4][:], in_=hg_in[q].rearrange("h k v -> k h v")), writes=[R[("sinf", b4)]], dma=R[("sinf", b4)], uwrites=ALIAS_D)

    for q in range(3):
        sload(q)
    def sprep(q):
        b, b4 = q % 2, q % 4
        S.op(ACT, lambda e: e.activation(out=sinb[b][:], in_=sinf[b4][:], func=AF.Copy), reads=[R[("sinf", b4)]], writes=[R[("sinb", b)]], uwrites=ALIAS_D)
        S.op(DVE, lambda e: e.tensor_scalar(out=kgm[b][:], in0=kgTM[:, 16, :], scalar1=rowm[:, q:q + 1], scalar2=None, op0=ALU.mult), reads=[R["kgTM"], Cst], writes=[R[("kgm", b)]], uwrites=ALIAS_D)

    sprep(0)
    for q in range(16):
        b = q % 2
        b4 = q % 4
        if q + 3 < 16:
            sload(q + 3)
        bkvs = []
        for h in range(4):
            bo, rbo = bos[h]
            S.op(PE, lambda e, bo=bo: e.matmul(bo[:, q * 8:(q + 1) * 8], lhsT=sinb[b][:, h, :], rhs=qT[:, h, 2048 + q * 8:2048 + (q + 1) * 8], start=False, stop=(q == 15)),
                 reads=[R[("sinb", b)], R[("qT", h)]], writes=[rbo])
            bkv, rbkv = poolbank(6, 2)
            S.op(PE, lambda e, bkv=bkv: e.matmul(bkv[:, 0:128], lhsT=kgm[b][:, h * 128:(h + 1) * 128], rhs=iv[:, 16, h * 128:(h + 1) * 128], start=True, stop=True),
                 reads=[R[("kgm", b)], R[("iv", 16)]], writes=[rbkv])
            S.op(DVE, lambda e, bkv=bkv: e.tensor_tensor(out=stmp[h][:], in0=bkv[:, 0:128], in1=sinf[b4][:, h, :], op=ALU.add), reads=[rbkv, R[("sinf", b4)]], writes=[R[("stmp", h)]])
            if h == 1 and q + 1 < 16:
                sprep(q + 1)
            S.op(ACT, lambda e: e.activation(out=sout[b][:, h, :], in_=stmp[h][:], func=AF.Copy, scale=eGl[:, h, 32 + q:33 + q]), reads=[R[("stmp", h)], R["eGl"]], writes=[R[("sout", b)]], uwrites=ALIAS_D)
        store(hgs_o[q].rearrange("h k v -> k h v"), sout[b][:], R[("sout", b)])
    for h in range(4):
        hg_post(16, h, *bos[h])

    stop_at("D")
    barrier("D", [R[("sinf", 0)], R[("sinf", 1)], R[("sinf", 2)], R[("sinf", 3)], R[("sout", 0)], R[("sout", 1)]] + [R[("Sf", h_)] for h_ in range(4)])
    cur[0] = mark_L1
    a_re = sb("a_re", [128, 16], F32)
    a_im = sb("a_im", [128, 16], F32)
    ldt = sb("ldt", [128, 16], F32)
    b_re = sb("b_re", [128, 16, 16], F32)
    b_im = sb("b_im", [128, 16, 16], F32)
    c_re = sb("c_re", [128, 16, 16], F32)
    c_im = sb("c_im", [128, 16, 16], F32)
    jrow = sb("jrow", [128, 256], F32)
    hin_re = sb("hin_re", [128, 16, 16], F32)
    hin_im = sb("hin_im", [128, 16, 16], F32)
    P5 = R["p5"]
    for t_, d_ in ((a_re, are_d), (a_im, aim_d), (ldt, ldt_d), (b_re, bre_d), (b_im, bim_d), (c_re, cre_d), (c_im, cim_d), (jrow, jrow_d), (hin_re, s5re_in), (hin_im, s5im_in)):
        load(t_[:], d_, P5)
    names = ["dt", "ard", "th", "mag", "t0", "t1", "cs", "sn", "den", "nr", "fre", "fim", "rho8", "phi", "nth"]
    V = {n_: sb("v_" + n_, [128, 16], F32) for n_ in names}
    pw0 = sb("pw0", [128, 4, 16], F32)
    pw1 = sb("pw1", [128, 4, 16], F32)
    APR = sb("APR", [128, 9, 16], F32)
    API = sb("API", [128, 9, 16], F32)
    NAPR = sb("NAPR", [128, 9, 16], F32)
    NAPI = sb("NAPI", [128, 9, 16], F32)
    bb_re = sb("bb_re", [128, 16, 16], F32)
    bb_im = sb("bb_im", [128, 16, 16], F32)
    tbb = sb("tbb", [128, 16, 16], F32)

    def pv(fn, reads=(), writes=()):
        S.op(DVE, fn, reads=[P5] + list(reads), writes=[P5] + list(writes))

    def pa(fn):
        S.op(ACT, fn, reads=[P5], writes=[P5])

    def tsc(o, i, s1, s2=None, op0=ALU.mult, op1=None):
        if op1 is None:
            pv(lambda e: e.tensor_scalar(out=o, in0=i, scalar1=s1, scalar2=None, op0=op0))
        else:
            pv(lambda e: e.tensor_scalar(out=o, in0=i, scalar1=s1, scalar2=s2, op0=op0, op1=op1))

    def tt(o, a, b_, op):
        pv(lambda e: e.tensor_tensor(out=o, in0=a, in1=b_, op=op))

    def sin_of(out, ang, shift):
        tsc(V["t0"][:], ang, 1.0 / TWO_PI, shift / TWO_PI, ALU.mult, ALU.add)
        tsc(V["t0"][:], V["t0"][:], MAGIC, None, ALU.add)
        tsc(V["t0"][:], V["t0"][:], -MAGIC, None, ALU.add)
        pv(lambda e: e.scalar_tensor_tensor(out=V["t1"][:], in0=V["t0"][:], scalar=-TWO_PI, in1=ang, op0=ALU.mult, op1=ALU.add))
        tsc(V["t1"][:], V["t1"][:], -PI_SAFE - shift, PI_SAFE - shift, ALU.max, ALU.min)
        pa(lambda e: e.activation(out=out, in_=V["t1"][:], func=AF.Sin, bias=shiftb[shift]))

    cq = sb("cq", [128, 1], F32)
    cm = sb("cm", [128, 1], F32)
    S.op(DVE, lambda e: e.memset(cq[:], 0.25), writes=[P5])
    S.op(DVE, lambda e: e.memset(cm[:], MAGIC), writes=[P5])
    ncm = sb("ncm", [128, 1], F32)
    S.op(DVE, lambda e: e.memset(ncm[:], -MAGIC), writes=[P5])
    shb0 = sb("shb0", [128, 1], F32)
    shb1 = sb("shb1", [128, 1], F32)
    S.op(DVE, lambda e: e.memset(shb0[:], 0.0), writes=[P5])
    S.op(DVE, lambda e: e.memset(shb1[:], math.pi / 2), writes=[P5])
    shiftb = {0.0: shb0[:, 0:1], math.pi / 2: shb1[:, 0:1]}

    pa(lambda e: e.activation(out=V["dt"][:], in_=ldt[:], func=AF.Exp))
    tt(V["ard"][:], a_re[:], V["dt"][:], ALU.mult)
    tt(V["th"][:], a_im[:], V["dt"][:], ALU.mult)
    pa(lambda e: e.activation(out=V["mag"][:], in_=V["ard"][:], func=AF.Exp))
    pa(lambda e: e.activation(out=V["rho8"][:], in_=V["ard"][:], func=AF.Exp, scale=8.0))
    sin_of(V["sn"][:], V["th"][:], 0.0)
    sin_of(V["cs"][:], V["th"][:], math.pi / 2)
    tsc(V["phi"][:], V["th"][:], 8.0 / TWO_PI)
    pv(lambda e: e.memset(APR[:, 0, :], 1.0))
    pv(lambda e: e.memset(API[:, 0, :], 0.0))
    tt(APR[:, 1, :], V["mag"][:], V["cs"][:], ALU.mult)
    tt(API[:, 1, :], V["mag"][:], V["sn"][:], ALU.mult)
    for n_ in (1, 2, 4):
        lo, hi = n_ + 1, 2 * n_ + 1
        bre_ = APR[:, n_:n_ + 1, :].to_broadcast([128, n_, 16])
        bim_ = API[:, n_:n_ + 1, :].to_broadcast([128, n_, 16])
        t0_, t1_ = pw0[:, 0:n_, :], pw1[:, 0:n_, :]
        tt(t0_, APR[:, 1:n_ + 1, :], bre_, ALU.mult)
        tt(t1_, API[:, 1:n_ + 1, :], bim_, ALU.mult)
        tt(APR[:, lo:hi, :], t0_, t1_, ALU.subtract)
        tt(t0_, APR[:, 1:n_ + 1, :], bim_, ALU.mult)
        tt(t1_, API[:, 1:n_ + 1, :], bre_, ALU.mult)
        tt(API[:, lo:hi, :], t0_, t1_, ALU.add)
    tsc(NAPR[:], APR[:], -1.0)
    tsc(NAPI[:], API[:], -1.0)
    tt(V["t0"][:], a_re[:], a_re[:], ALU.mult)
    tt(V["t1"][:], a_im[:], a_im[:], ALU.mult)
    tt(V["den"][:], V["t0"][:], V["t1"][:], ALU.add)
    pv(lambda e: e.reciprocal(out=V["den"][:], in_=V["den"][:]))
    tsc(V["nr"][:], APR[:, 1, :], -1.0, None, ALU.add)
    tt(V["t0"][:], V["nr"][:], a_re[:], ALU.mult)
    tt(V["t1"][:], API[:, 1, :], a_im[:], ALU.mult)
    tt(V["t0"][:], V["t0"][:], V["t1"][:], ALU.add)
    tt(V["fre"][:], V["t0"][:], V["den"][:], ALU.mult)
    tt(V["t0"][:], API[:, 1, :], a_re[:], ALU.mult)
    tt(V["t1"][:], V["nr"][:], a_im[:], ALU.mult)
    tt(V["t0"][:], V["t0"][:], V["t1"][:], ALU.subtract)
    tt(V["fim"][:], V["t0"][:], V["den"][:], ALU.mult)
    bc16 = lambda v: v.unsqueeze(2).to_broadcast([128, 16, 16])
    tt(bb_re[:], b_re[:], bc16(V["fre"][:]), ALU.mult)
    tt(tbb[:], b_im[:], bc16(V["fim"][:]), ALU.mult)
    tt(bb_re[:], bb_re[:], tbb[:], ALU.subtract)
    tt(bb_im[:], b_im[:], bc16(V["fre"][:]), ALU.mult)
    tt(tbb[:], b_re[:], bc16(V["fim"][:]), ALU.mult)
    tt(bb_im[:], bb_im[:], tbb[:], ALU.add)

    stop_at("Cparam")
    CALall = sb("CALall", [128, 8, 9, 2, 128], BF16)
    ZZall = sb("ZZall", [128, 4, 8, 2, 128], BF16)
    CAL = [CALall[:, i] for i in range(8)]
    ZZ = [ZZall[:, i] for i in range(4)]
    WBL = [sb("WBL%d" % i, [128, 8, 2, 128], BF16) for i in range(4)]
    tca = sb("tca", [128, 4, 9, 16], F32)
    S.op(POOL, lambda e: e.memset(CALall[:], 0.0), writes=[R[("CAL", i)] for i in range(8)])
    S.op(POOL, lambda e: e.memset(ZZall[:], 0.0), writes=[R[("ZZ", i)] for i in range(4)])
    Kblk = sb("Kblk", [128, 8, 128], BF16)
    Hp_re = sb("Hp_re", [128, 8, NB], BF16)
    Hp_im = sb("Hp_im", [128, 8, NB], BF16)
    cosT = [sb("cosT%d" % i, [128, 256], F32) for i in range(2)]
    sinT = [sb("sinT%d" % i, [128, 256], F32) for i in range(2)]
    ang = sb("ang", [128, 256], F32)
    ang2 = [sb("ang2_%d" % i, [128, 256], F32) for i in range(2)]
    rhoR = sb("rhoR", [128, 256], F32)
    gin_re = sb("gin_re", [128, 256], F32)
    gin_im = sb("gin_im", [128, 256], F32)
    g_re = [sb("g_re%d" % i, [128, 256], F32) for i in range(2)]
    g_im = [sb("g_im%d" % i, [128, 256], F32) for i in range(2)]
    tA = sb("tA", [128, 256], F32)
    tB = sb("tB", [128, 256], F32)
    pA = sb("pA", [128, 256], F32)
    pB = sb("pB", [128, 256], F32)
    hf_re = sb("hf_re", [128, 256], F32)
    hf_im = sb("hf_im", [128, 256], F32)
    hfin_re = sb("hfin_re", [128, 16], F32)
    hfin_im = sb("hfin_im", [128, 16], F32)
    hnew_re = sb("hnew_re", [128, 16, 16], F32)
    hnew_im = sb("hnew_im", [128, 16, 16], F32)
    ts1 = sb("ts1", [128, 16], F32)
    gbuf = sb("gbuf", [128, 4, NT], BF16)
    ytmp = [sb("ytmp%d" % i, [128, NB], F32) for i in range(2)]
    wglu = sb("wglu", [128, 4, 512], BF16)
    sgm = [sb("sgm%d" % i, [128, 512], BF16) for i in range(2)]
    load(wglu[:], w_glu.rearrange("(c p) n -> p c n", p=128), R["wglu"], eng=POOL)
    S.op(POOL, lambda e: e.memset(Hp_re[:, :, 0:1], 0.0), writes=[R["Hp0"]])
    S.op(POOL, lambda e: e.memset(Hp_im[:, :, 0:1], 0.0), writes=[R["Hp0"]])

    def bc_p(v, n_):
        return v.unsqueeze(2).to_broadcast([64, n_, 16])

    def bc_c(v, n_):
        return v.unsqueeze(1).to_broadcast([64, n_, 16])

    def s5_poolbuild(T, parts=(0, 1, 2, 3), eng=POOL):
        sbase = (T % 2) * 4
        rcals = [R[("CAL", sbase + b_)] for b_ in range(4)]
        rzzs = [R[("ZZ", b_)] for b_ in range(4)]
        CROW, ZROW = 8 * 9 * 2 * 128, 4 * 8 * 2 * 128
        for g2 in range(2):
            ps_ = slice(64 * g2, 64 * g2 + 64)
            rt = R[("tca", g2)]

            def coef(tab, npow):
                return bass.AP(tensor=tab, offset=64 * g2 * 144 + T * 4, ap=[[144, 64], [1, 4], [16, npow], [0, 16]])

            def cal_out(ri):
                return bass.AP(tensor=CALall, offset=64 * g2 * CROW + sbase * 2304 + ri * 128 + g2 * 16, ap=[[CROW, 64], [2336, 4], [256, 9], [1, 16]])

            def zz_out(ri):
                return bass.AP(tensor=ZZall, offset=64 * g2 * ZROW + ri * 128 + g2 * 16, ap=[[ZROW, 64], [2080, 4], [256, 8], [1, 16]])

            def bsrc(x, npow):
                return x[ps_, T * 4:(T + 1) * 4, :].unsqueeze(2).to_broadcast([64, 4, npow, 16])

            for (ri, s_a, s_b) in (((0, APR, NAPI), (1, NAPI, NAPR)) if (g2 * 2) in parts else ()):
                o_t, o_c = tca[ps_, :, :, :], cal_out(ri)
                i0a, i1a, i0b, i1b = bsrc(c_re, 9), coef(s_a, 9), bsrc(c_im, 9), coef(s_b, 9)
                S.op(eng, lambda e: e.tensor_tensor(out=o_t, in0=i0a, in1=i1a, op=ALU.mult), reads=[P5], writes=[rt])
                S.op(eng, lambda e: e.tensor_tensor(out=o_c, in0=i0b, in1=i1b, op=ALU.mult), reads=[P5], writes=rcals)
                S.op(eng, lambda e: e.tensor_tensor(out=o_c, in0=o_c, in1=o_t, op=ALU.add), reads=[rt] + rcals, writes=rcals)
            for (ri, x_a, s_a, x_b, s_b) in (((0, bb_re, APR, bb_im, NAPI), (1, bb_im, APR, bb_re, API)) if (g2 * 2 + 1) in parts else ()):
                o_t, o_z = tca[ps_, :, 0:8, :], zz_out(ri)
                i0a, i1a, i0b, i1b = bsrc(x_a, 8), coef(s_a, 8), bsrc(x_b, 8), coef(s_b, 8)
                S.op(eng, lambda e: e.tensor_tensor(out=o_t, in0=i0a, in1=i1a, op=ALU.mult), reads=[P5], writes=[rt])
                S.op(eng, lambda e: e.tensor_tensor(out=o_z, in0=i0b, in1=i1b, op=ALU.mult), reads=[P5], writes=rzzs)
                S.op(eng, lambda e: e.tensor_tensor(out=o_z, in0=o_z, in1=o_t, op=ALU.add), reads=[rt] + rzzs, writes=rzzs)

    def s5_wbl(T):
        for b in range(4):
            rzz, rwbl = R[("ZZ", b)], R[("WBL", b)]
            for half in range(4):
                bk, rb = nextbank()
                pst = bk[:].bitcast(BF16)
                for k4 in range(4):
                    idx = half * 4 + k4
                    p, ri = idx // 2, idx % 2
                    S.op(PE, lambda e, pst=pst, k4=k4, p=p, ri=ri, b=b: e.transpose(out=pst[:, k4 * 128:(k4 + 1) * 128], in_=ZZ[b][:, p, ri, :], identity=ident[:]),
                         reads=[rzz, Cst], writes=[rb])
                copy_op(alt_eng(), WBL[b][:, half * 2:half * 2 + 2, :, :].rearrange("p a r n -> p (a r n)"), pst[:, 0:512], [rb], [rwbl])

    def s5_taps(T):
        kb = []
        for hf in range(2):
            bk, rb = nextbank()
            kb.append((bk, rb))
            n_mm = 0
            for b in range(4):
                for ri in range(2):
                    S.op(PE, lambda e, bk=bk, b=b, ri=ri, hf=hf, n_mm=n_mm: e.matmul(bk[:, :], lhsT=ZZ[b][:, 0, ri, :], rhs=CAL[(T % 2) * 4 + b][:, hf * 4:hf * 4 + 4, ri, :], start=(n_mm == 0), stop=(n_mm == 7)),
                         reads=[R[("ZZ", b)], R[("CAL", (T % 2) * 4 + b)]], writes=[rb])
                    n_mm += 1
            copy_op(alt_eng(), Kblk[:, hf * 4:hf * 4 + 4, :].rearrange("p a n -> p (a n)"), bk[:, :], [rb], [R["Kblk"]])

    def s5_tables(gg):
        pb = gg % 2
        Rang, Rtab = R["ang"], R[("tab", pb)]
        cT, sT = cosT[pb], sinT[pb]
        bc256 = lambda v: v.to_broadcast([128, 256])
        a2s, a2c = ang2[0], ang2[1]
        Rs, Rc = R[("ang2", 0)], R[("ang2", 1)]
        S.op(DVE, lambda e: e.tensor_tensor(out=ang[:], in0=jrow[:], in1=bc256(V["phi"][:, gg:gg + 1]), op=ALU.mult), reads=[P5], writes=[Rang])
        S.op(ACT, lambda e: e.activation(out=a2s[:], in_=ang[:], func=AF.Identity, bias=cm[:, 0:1]), reads=[Rang, P5], writes=[Rs])
        S.op(ACT, lambda e: e.activation(out=a2s[:], in_=a2s[:], func=AF.Identity, bias=ncm[:, 0:1]), reads=[Rs, P5], writes=[Rs])
        S.op(ACT, lambda e: e.activation(out=a2c[:], in_=ang[:], func=AF.Identity, bias=cq[:, 0:1]), reads=[Rang, P5], writes=[Rc])
        S.op(ACT, lambda e: e.activation(out=a2c[:], in_=a2c[:], func=AF.Identity, bias=cm[:, 0:1]), reads=[Rc, P5], writes=[Rc])
        S.op(ACT, lambda e: e.activation(out=a2c[:], in_=a2c[:], func=AF.Identity, bias=ncm[:, 0:1]), reads=[Rc, P5], writes=[Rc])
        S.op(DVE, lambda e: e.tensor_tensor(out=a2s[:], in0=ang[:], in1=a2s[:], op=ALU.subtract), reads=[Rs, Rang], writes=[Rs])
        S.op(DVE, lambda e: e.tensor_tensor(out=a2c[:], in0=ang[:], in1=a2c[:], op=ALU.subtract), reads=[Rc, Rang], writes=[Rc])
        S.op(ACT, lambda e: e.activation(out=sT[:], in_=a2s[:], func=AF.Sin, bias=shiftb[0.0], scale=6.283185), reads=[Rs, P5], writes=[Rtab])
        S.op(ACT, lambda e: e.activation(out=cT[:], in_=a2c[:], func=AF.Sin, bias=shiftb[math.pi / 2], scale=6.283185), reads=[Rc, P5], writes=[Rtab])

    def s5_scan(T, build_next=False):
        s5_tables(T * 4)
        for b in range(4):
            gg = T * 4 + b
            pb = gg % 2
            slot = (T % 2) * 4 + b
            rwbl = R[("WBL", b)]
            bS = []
            for ri in range(2):
                bk, rb = poolbank(0, 4)
                bS.append((bk, rb))
                for s_ in range(8):
                    S.op(PE, lambda e, bk=bk, ri=ri, s_=s_: e.matmul(bk[:, 0:NB], lhsT=WBL[b][:, 7 - s_, ri, :], rhs=uP[:, T, s_ * NB:(s_ + 1) * NB], start=(s_ == 0), stop=(s_ == 7)),
                         reads=[rwbl, R[("uP", T)]], writes=[rb])
            (bre, rbre), (bim, rbim) = bS
            Rang, Rtab, RtAB, Rgin, Rg, Rrho, RpAB, Rhf, Rts = R["ang"], R[("tab", pb)], R["tAB"], R["gin"], R[("g", pb)], R["rho"], R["pAB"], R["hf"], R["ts"]
            cT, sT, gr, gi = cosT[pb], sinT[pb], g_re[pb], g_im[pb]
            bc256 = lambda v: v.to_broadcast([128, 256])
            if b + 1 < 4:
                s5_tables(T * 4 + b + 1)
            S.op(DVE, lambda e: e.tensor_tensor(out=tA[:], in0=bre[:, 0:256], in1=cT[:], op=ALU.mult), reads=[rbre, Rtab], writes=[RtAB])
            S.op(DVE, lambda e: e.tensor_tensor(out=tB[:], in0=bim[:, 0:256], in1=sT[:], op=ALU.mult), reads=[rbim, Rtab], writes=[RtAB])
            S.op(DVE, lambda e: e.tensor_tensor(out=gin_re[:], in0=tA[:], in1=tB[:], op=ALU.add), reads=[RtAB], writes=[Rgin])
            S.op(DVE, lambda e: e.tensor_tensor(out=tA[:], in0=bim[:, 0:256], in1=cT[:], op=ALU.mult), reads=[rbim, Rtab, RtAB], writes=[RtAB])
            S.op(DVE, lambda e: e.tensor_tensor(out=tB[:], in0=bre[:, 0:256], in1=sT[:], op=ALU.mult), reads=[rbre, Rtab], writes=[RtAB])
            S.op(DVE, lambda e: e.tensor_tensor(out=gin_im[:], in0=tA[:], in1=tB[:], op=ALU.subtract), reads=[RtAB], writes=[Rgin])
            S.op(DVE, lambda e: e.tensor_tensor_scan(out=gr[:], data0=bc256(V["rho8"][:, gg:gg + 1]), data1=gin_re[:], initial=0.0, op0=ALU.mult, op1=ALU.add), reads=[P5, Rgin], writes=[Rg])
            S.op(DVE, lambda e: e.tensor_tensor_scan(out=gi[:], data0=bc256(V["rho8"][:, gg:gg + 1]), data1=gin_im[:], initial=0.0, op0=ALU.mult, op1=ALU.add), reads=[P5, Rgin], writes=[Rg])
            S.op(POOL, lambda e: e.tensor_tensor(out=pA[:], in0=gr[:], in1=cT[:], op=ALU.mult), reads=[Rg, Rtab], writes=[RpAB])
            S.op(POOL, lambda e: e.tensor_tensor(out=pB[:], in0=gi[:], in1=sT[:], op=ALU.mult), reads=[Rg, Rtab], writes=[RpAB])
            S.op(POOL, lambda e: e.tensor_tensor(out=hf_re[:], in0=pA[:], in1=pB[:], op=ALU.subtract), reads=[RpAB], writes=[Rhf])
            S.op(DVE, lambda e: e.tensor_tensor(out=tA[:], in0=gr[:], in1=sT[:], op=ALU.mult), reads=[Rg, Rtab, RtAB], writes=[RtAB])
            S.op(DVE, lambda e: e.tensor_tensor(out=tB[:], in0=gi[:], in1=cT[:], op=ALU.mult), reads=[Rg, Rtab], writes=[RtAB])
            S.op(DVE, lambda e: e.tensor_tensor(out=hf_im[:], in0=tA[:], in1=tB[:], op=ALU.add), reads=[RtAB], writes=[R["hfi"]])
            if build_next:
                s5_poolbuild(T + 1, parts=(b,))
            rHp = R[("Hp", slot)]
            S.op(ACT, lambda e: e.activation(out=Hp_re[:, slot, 1:256], in_=hf_re[:, 0:255], func=AF.Copy), reads=[Rhf, R["Hp0"]], writes=[rHp])
            S.op(ACT, lambda e: e.activation(out=Hp_im[:, slot, 1:256], in_=hf_im[:, 0:255], func=AF.Copy), reads=[R["hfi"]], writes=[rHp])
            S.op(ACT, lambda e: e.activation(out=hfin_re[:, gg:gg + 1], in_=hf_re[:, 255:256], func=AF.Copy), reads=[Rhf], writes=[R["hfin"]])
            S.op(ACT, lambda e: e.activation(out=hfin_im[:, gg:gg + 1], in_=hf_im[:, 255:256], func=AF.Copy), reads=[R["hfi"]], writes=[R["hfin"]])
            S.op(ACT, lambda e: e.activation(out=Hp_re[:, slot, 256:NB], in_=hin_re[:, gg, :], func=AF.Copy), reads=[P5], writes=[rHp])
            S.op(ACT, lambda e: e.activation(out=Hp_im[:, slot, 256:NB], in_=hin_im[:, gg, :], func=AF.Copy), reads=[P5], writes=[rHp])
            S.op(DVE, lambda e: e.scalar_tensor_tensor(out=ts1[:], in0=hin_im[:, gg, :], scalar=NAPI[:, 8, gg:gg + 1], in1=bre[:, 256:NB], op0=ALU.mult, op1=ALU.add), reads=[P5, rbre], writes=[Rts])
            S.op(DVE, lambda e: e.scalar_tensor_tensor(out=hnew_re[:, gg, :], in0=hin_re[:, gg, :], scalar=APR[:, 8, gg:gg + 1], in1=ts1[:], op0=ALU.mult, op1=ALU.add), reads=[P5, Rts], writes=[R["hnew"]])
            S.op(DVE, lambda e: e.scalar_tensor_tensor(out=ts1[:], in0=hin_re[:, gg, :], scalar=API[:, 8, gg:gg + 1], in1=bim[:, 256:NB], op0=ALU.mult, op1=ALU.add), reads=[P5, rbim, Rts], writes=[Rts])
            S.op(DVE, lambda e: e.scalar_tensor_tensor(out=hnew_im[:, gg, :], in0=hin_im[:, gg, :], scalar=APR[:, 8, gg:gg + 1], in1=ts1[:], op0=ALU.mult, op1=ALU.add), reads=[P5, Rts], writes=[R["hnew"]])

    def s5_conv(T):
        for i in range(8):
            bk, rb = nextbank()
            mm = []
            for tau in range(i + 1):
                mm.append((Kblk[:, tau, :], uP[:, T, (i - tau) * NB:(i - tau + 1) * NB], [R["Kblk"], R[("uP", T)]]))
            for b in range(4):
                gg = T * 4 + b
                mm.append((CAL[(T % 2) * 4 + b][:, i + 1, 0, :], Hp_re[:, (T % 2) * 4 + b, :], [R[("CAL", (T % 2) * 4 + b)], R[("Hp", (T % 2) * 4 + b)], R["Hp0"]]))
                mm.append((CAL[(T % 2) * 4 + b][:, i + 1, 1, :], Hp_im[:, (T % 2) * 4 + b, :], [R[("CAL", (T % 2) * 4 + b)], R[("Hp", (T % 2) * 4 + b)], R["Hp0"]]))
            for k_, (l_, r_, rd_) in enumerate(mm):
                S.op(PE, lambda e, bk=bk, l_=l_, r_=r_, k_=k_, last=len(mm) - 1: e.matmul(bk[:, 0:NB], lhsT=l_, rhs=r_, start=(k_ == 0), stop=(k_ == last)), reads=rd_, writes=[rb])
            yb = i % 2
            S.op(DVE, lambda e, bk=bk, yb=yb, i=i, T=T: e.scalar_tensor_tensor(out=ytmp[yb][:], in0=uP[:, T, i * NB:(i + 1) * NB], scalar=dsk[:, T:T + 1], in1=bk[:, 0:NB], op0=ALU.mult, op1=ALU.add),
                 reads=[rb, R[("uP", T)], Cst], writes=[R[("ytmp", yb)]])
            S.op(ACT, lambda e, yb=yb, i=i, T=T: e.activation(out=gbuf[:, T, :].rearrange("p (j i) -> p i j", i=8)[:, i, :], in_=ytmp[yb][:], func=AF.Gelu),
                 reads=[R[("ytmp", yb)]], writes=[R[("gbuf", T)]])

    s5_poolbuild(0, parts=(0, 1), eng=POOL)
    s5_poolbuild(0, parts=(2, 3), eng=DVE)
    s5_wbl(0)
    s5_taps(0)
    for T in range(4):
        s5_scan(T, build_next=(T + 1 < 4))
        if T + 1 < 4:
            s5_wbl(T + 1)
        s5_conv(T)
        if T + 1 < 4:
            s5_taps(T + 1)
    store(s5p_re_o, hfin_re[:], R["hfin"])
    store(s5p_im_o, hfin_im[:], R["hfin"])
    store(s5s_re_o, hnew_re[:], R["hnew"])
    store(s5s_im_o, hnew_im[:], R["hnew"])
    stop_at("C")
    for ct in range(4):
        for (c0, n) in CHUNKS:
            bk, rb = nextbank()
            for kc in range(4):
                S.op(PE, lambda e, bk=bk, kc=kc, ct=ct, c0=c0, n=n: e.matmul(bk[:, 0:n], lhsT=wglu[:, kc, ct * 128:(ct + 1) * 128], rhs=gbuf[:, kc, c0:c0 + n], start=(kc == 0), stop=(kc == 3)),
                     reads=[R["wglu"]] + [R[("gbuf", k)] for k in range(4)], writes=[rb])
            sb_i = (c0 // 512) % 2
            S.op(ACT, lambda e, bk=bk, sb_i=sb_i, n=n: e.activation(out=sgm[sb_i][:, 0:n], in_=bk[:, 0:n], func=AF.Sigmoid), reads=[rb], writes=[R[("sgm", sb_i)]])
            S.op(DVE, lambda e, sb_i=sb_i, ct=ct, c0=c0, n=n: e.tensor_tensor(out=mixT[:, ct, c0:c0 + n], in0=gbuf[:, ct, c0:c0 + n], in1=sgm[sb_i][:, 0:n], op=ALU.mult),
                 reads=[R[("sgm", sb_i)], R[("gbuf", ct)]], writes=[R[("mix", t_)] for t_ in range(c0 // 128, (c0 + n) // 128)])

    stop_at("Cglu")
    barrier("C", [P5, R["wglu"], R["hfin"], R["hnew"]])
    cur[0] = allocs["uP"][0]
    x1 = sb("x1", [128, NTILE, 1024], F32)
    actT = sb("actT", [128, 11, NT], BF16)
    wout = sb("wout", [128, 8, 1024], BF16, at=allocs["actT"][0])
    wd = sb("wd", [128, 11, 1024], BF16)
    for hf_ in range(2):
        load(wout[:, :, hf_ * 512:(hf_ + 1) * 512], w_out[:, hf_ * 512:(hf_ + 1) * 512].rearrange("(c p) n -> p c n", p=128), R[("wout", hf_)], eng=POOL)
    xnb2 = [sb("xnb2_%d" % i, [128, 1024], BF16) for i in range(2)]
    xnb[0], xnb[1] = xnb2[0], xnb2[1]
    NXB[0] = 2
    ostg = [sb("ostg%d" % i, [128, 1024], F32) for i in range(2)]
    wg = [sb("wg%d" % i, [128, 8, 128], BF16) for i in range(2)]
    wu = [sb("wu%d" % i, [128, 8, 128], BF16) for i in range(2)]
    sgt = [sb("sgt%d" % i, [128, 512], BF16) for i in range(2)]
    def mmaddE(t):
        load(x1[:, t, :], xin[t * 128:(t + 1) * 128, :], R[("x1", t)])
        for hf in range(2):
            bk, rb = nextbank()
            for kc in range(8):
                S.op(PE, lambda e, bk=bk, kc=kc, t=t, hf=hf: e.matmul(bk[:, :], lhsT=mixT[:, kc, t * 128:(t + 1) * 128], rhs=wout[:, kc, hf * 512:(hf + 1) * 512], start=(kc == 0), stop=(kc == 7)),
                     reads=[R[("mix", t)], R[("wout", hf)]], writes=[rb])
            S.op(DVE, lambda e, bk=bk, t=t, hf=hf: e.tensor_tensor(out=x1[:, t, hf * 512:(hf + 1) * 512], in0=bk[:, :], in1=x1[:, t, hf * 512:(hf + 1) * 512], op=ALU.add),
                 reads=[rb, R[("x1", t)]], writes=[R[("x1", t)]])

    mmaddE(0)
    norm_stats(0, x1[:, 0, :], R[("x1", 0)])
    for t in range(NTILE):
        if t + 1 < NTILE:
            mmaddE(t + 1)
        norm_trans(t, gffn, mixT, "mix")
        if t + 1 < NTILE:
            norm_stats(t + 1, x1[:, t + 1, :], R[("x1", t + 1)])

    stop_at("E")
    for half in range(2):
        load(wd[:], w_down[half * 1408:(half + 1) * 1408, :].rearrange("(c p) n -> p c n", p=128), R["wd"], eng=POOL)
        for f in range(11):
            ft = half * 11 + f
            wb = ft % 2
            load(wg[wb][:], w_gate[:, ft * 128:(ft + 1) * 128].rearrange("(c p) n -> p c n", p=128), R[("wg", wb)], eng=POOL)
            load(wu[wb][:], w_up[:, ft * 128:(ft + 1) * 128].rearrange("(c p) n -> p c n", p=128), R[("wu", wb)], eng=POOL)
            for (c0, n) in CHUNKS:
                mres = [R[("mix", t_)] for t_ in range(c0 // 128, (c0 + n) // 128)]
                bg, rbg = nextbank()
                for kc in range(8):
                    S.op(PE, lambda e, bg=bg, kc=kc, wb=wb, c0=c0, n=n: e.matmul(bg[:, 0:n], lhsT=wg[wb][:, kc, :], rhs=mixT[:, kc, c0:c0 + n], start=(kc == 0), stop=(kc == 7)),
                         reads=[R[("wg", wb)]] + mres, writes=[rbg])
                bu, rbu = nextbank()
                for kc in range(8):
                    S.op(PE, lambda e, bu=bu, kc=kc, wb=wb, c0=c0, n=n: e.matmul(bu[:, 0:n], lhsT=wu[wb][:, kc, :], rhs=mixT[:, kc, c0:c0 + n], start=(kc == 0), stop=(kc == 7)),
                         reads=[R[("wu", wb)]] + mres, writes=[rbu])
                sb_i = (c0 // 512) % 2
                S.op(ACT, lambda e, bg=bg, sb_i=sb_i, n=n: e.activation(out=sgt[sb_i][:, 0:n], in_=bg[:, 0:n], func=AF.Silu), reads=[rbg], writes=[R[("sgt", sb_i)]])
                S.op(DVE, lambda e, bu=bu, sb_i=sb_i, f=f, c0=c0, n=n: e.tensor_tensor(out=actT[:, f, c0:c0 + n], in0=bu[:, 0:n], in1=sgt[sb_i][:, 0:n], op=ALU.mult),
                     reads=[rbu, R[("sgt", sb_i)]], writes=[R[("wout", 0)], R[("wout", 1)]] + [R[("actT", t_)] for t_ in range(c0 // 128, (c0 + n) // 128)])
        for t in range(NTILE):
            for hf in range(2):
                bk, rb = nextbank()
                for f in range(11):
                    S.op(PE, lambda e, bk=bk, f=f, t=t, hf=hf: e.matmul(bk[:, :], lhsT=actT[:, f, t * 128:(t + 1) * 128], rhs=wd[:, f, hf * 512:(hf + 1) * 512], start=(f == 0), stop=(f == 10)),
                         reads=[R[("actT", t)], R["wd"]], writes=[rb])
                S.op(DVE, lambda e, bk=bk, t=t, hf=hf: e.tensor_tensor(out=x1[:, t, hf * 512:(hf + 1) * 512], in0=bk[:, :], in1=x1[:, t, hf * 512:(hf + 1) * 512], op=ALU.add),
                     reads=[rb, R[("x1", t)]], writes=[R[("x1", t)]])
            if half == 1:
                b = t % 2
                ss = ssb[:, 2 + b:3 + b]
                rs = R[("ssf", b)]
                S.op(DVE, lambda e, ss=ss: e.memset(ss, 0.0), writes=[rs])
                S.op(ACT, lambda e, b=b, t=t, ss=ss: e.activation(out=ostg[b][:], in_=x1[:, t, :], func=AF.Square, accum_out=ss), reads=[R[("x1", t)]], writes=[R[("ostg", b)], rs])
                S.op(ACT, lambda e, ss=ss: e.activation(out=ss, in_=ss, func=AF.Sqrt, scale=1.0 / 1024, bias=epsb[:, 0:1]), reads=[rs, Cst], writes=[rs])
                S.op(DVE, lambda e, ss=ss: e.reciprocal(out=ss, in_=ss), reads=[rs], writes=[rs])
                S.op(DVE, lambda e, b=b, t=t, ss=ss: e.scalar_tensor_tensor(out=ostg[b][:], in0=x1[:, t, :], scalar=ss, in1=nfin[:], op0=ALU.mult, op1=ALU.mult),
                     reads=[rs, R[("x1", t)], Cst], writes=[R[("ostg", b)]])
                store(y_o[t * 128:(t + 1) * 128, :], ostg[b][:], R[("ostg", b)])


_NC_CACHE = {}


def _consts():
    bf = ml_dtypes.bfloat16
    s = np.arange(128)[:, None]
    t = np.arange(128)[None, :]
    m64 = ((s <= t) & (s // 64 == t // 64)).astype(bf)
    m8 = ((s <= t) & (s // 8 == t // 8)).astype(bf)
    rmask = np.ones((128, NT), np.float32)
    rmask[:, 0:2048:64] = 0.0
    rmask[:, 2048::8] = 0.0
    rowm = (np.arange(128)[:, None] // 8 == np.arange(16)[None, :]).astype(np.float32)
    jrow = np.broadcast_to(np.arange(256, dtype=np.float32)[None, :], (128, 256)).copy()
    return dict(ident=np.eye(128).astype(bf), m64=m64, m8=m8, rmask=rmask, rowm=rowm, jrow=jrow)


def _pair(a):
    sh = a.shape
    a = a.reshape(16, 2, 64, *sh[2:])
    a = np.moveaxis(a, 0, 2)
    return np.ascontiguousarray(a.reshape(128, 16, *sh[2:]))


def _unpair(a):
    sh = a.shape
    a = a.reshape(2, 64, 16, *sh[2:])
    a = np.moveaxis(a, 2, 0)
    return np.ascontiguousarray(a.reshape(32, 64, *sh[2:]))


def kernel(x_prompt, x_sample, state_s5_re, state_s5_im, state_hgrn, lb_param, norm_mix, w_in,
           s5_a_re, s5_a_im, s5_log_dt, s5_b_re, s5_b_im, s5_c_re, s5_c_im, s5_d, s5_w_glu,
           hg_norm, w_out, norm_ffn, w_gate, w_up, w_down, norm_final):
    f = lambda a: np.ascontiguousarray(np.asarray(a, dtype=np.float32))
    if "nc" not in _NC_CACHE:
        _NC_CACHE["nc"] = build_nc()
    nc = _NC_CACHE["nc"]
    cst = _consts()
    x_prompt, x_sample = f(x_prompt), f(x_sample)
    st_re, st_im, st_hg = f(state_s5_re)[0], f(state_s5_im)[0], f(state_hgrn)[0]
    shared = dict(
        w_in=f(w_in)[0], w_glu=f(s5_w_glu)[0], w_out=f(w_out)[0], w_gate=f(w_gate)[0], w_up=f(w_up)[0], w_down=f(w_down)[0],
        gmix=np.ascontiguousarray(f(norm_mix)[0].reshape(8, 128).T), gffn=np.ascontiguousarray(f(norm_ffn)[0].reshape(8, 128).T),
        nfin=f(norm_final), hgn=f(hg_norm)[0].reshape(128, 1),
        lbp=np.ascontiguousarray(f(lb_param).reshape(2, 4, 128).transpose(2, 0, 1).reshape(128, 8)),
        a_re=_pair(f(s5_a_re)[0]), a_im=_pair(f(s5_a_im)[0]),
        ldt=_pair(np.ascontiguousarray(np.broadcast_to(f(s5_log_dt)[0][:, None], (32, 64)))),
        b_re=_pair(f(s5_b_re)[0]), b_im=_pair(f(s5_b_im)[0]),
        c_re=_pair(np.ascontiguousarray(f(s5_c_re)[0].transpose(0, 2, 1))), c_im=_pair(np.ascontiguousarray(f(s5_c_im)[0].transpose(0, 2, 1))),
        dsk=np.ascontiguousarray(f(s5_d)[0].reshape(4, 128).T),
        **cst)
    in_maps = []
    for c in range(8):
        xin = np.concatenate([x_prompt[c], x_sample[16 * c:16 * c + 16].reshape(128, 1024)], axis=0)
        sre = np.ascontiguousarray(np.moveaxis(st_re[16 * c:16 * c + 16], 0, -1))
        sim = np.ascontiguousarray(np.moveaxis(st_im[16 * c:16 * c + 16], 0, -1))
        m = dict(shared)
        m.update(xin=np.ascontiguousarray(xin), s5re_in=_pair(sre), s5im_in=_pair(sim), hg_in=np.ascontiguousarray(st_hg[16 * c:16 * c + 16]))
        in_maps.append(m)
    res = run_bass_kernel_spmd(nc, in_maps, core_ids=list(range(8)))
    rs = res.results
    y_prompt = np.stack([rs[c]["y"][:2048] for c in range(8)])
    y_sample = np.concatenate([rs[c]["y"][2048:].reshape(16, 8, 1024) for c in range(8)])
    p_re = np.stack([_unpair(rs[c]["s5p_re"]) for c in range(8)])[None]
    p_im = np.stack([_unpair(rs[c]["s5p_im"]) for c in range(8)])[None]
    p_hg = np.stack([rs[c]["hgp"] for c in range(8)])[None]
    s_re = np.concatenate([np.moveaxis(_unpair(rs[c]["s5s_re"]), -1, 0) for c in range(8)])[None]
    s_im = np.concatenate([np.moveaxis(_unpair(rs[c]["s5s_im"]), -1, 0) for c in range(8)])[None]
    s_hg = np.concatenate([rs[c]["hgs"] for c in range(8)])[None]
    out = (y_prompt, y_sample, p_re, p_im, p_hg, s_re, s_im, s_hg)
    return tuple(np.ascontiguousarray(o, dtype=np.float32) for o in out)
```

```python
import math
import numpy as np
import ml_dtypes
from contextlib import ExitStack
import concourse.bass as bass
import concourse.mybir as mybir
from concourse.bass_utils import run_bass_kernel_spmd

F32 = mybir.dt.float32
BF16 = mybir.dt.bfloat16
AF = mybir.ActivationFunctionType
ALU = mybir.AluOpType
PE, ACT, DVE, POOL, SP = "tensor", "scalar", "vector", "gpsimd", "sync"
ENGS = [PE, ACT, DVE, POOL, SP]
NT, NTP, NTILE, NB = 2176, 2048, 17, 272
CHUNKS = [(0, 512), (512, 512), (1024, 512), (1536, 512), (2048, 128)]
TWO_PI = 2.0 * math.pi
MAGIC = 12582912.0
PI_SAFE = 3.14159


class Res:
    __slots__ = ("name", "w", "r", "dsem", "dcnt", "ws")

    def __init__(self, name=""):
        self.name = name
        self.w = None
        self.ws = []
        self.r = {}
        self.dsem = None
        self.dcnt = 0


class Op:
    __slots__ = ("eng", "fn", "deps", "dma_res", "dma_val", "milestone", "val", "ddeps")

    def __init__(self, eng, fn):
        self.eng = eng
        self.fn = fn
        self.deps = []
        self.ddeps = []
        self.dma_res = None
        self.dma_val = 0
        self.milestone = False
        self.val = 0


import types


def _freeze(fn):
    if fn.__closure__ is None:
        return fn
    cells = []
    for c in fn.__closure__:
        try:
            cells.append(types.CellType(c.cell_contents))
        except ValueError:
            cells.append(c)
    return types.FunctionType(fn.__code__, fn.__globals__, fn.__name__, fn.__defaults__, tuple(cells))


class Sched:
    def __init__(self, nc):
        self.nc = nc
        self.streams = {e: [] for e in ENGS}
        self.dma_resources = []
        self.pending = {e: [] for e in ENGS}
        self.last_dma = {}

    def barrier(self, extra_res=()):
        lasts = [self.streams[e][-1] for e in ENGS if self.streams[e]]
        lasts += [self.last_dma[id(r)] for r in extra_res if id(r) in self.last_dma]
        for e in ENGS:
            self.pending[e] = list(lasts)

    def _dep_on(self, op, prev, same_ok):
        if prev is None:
            return
        if prev.dma_res is not None:
            op.ddeps.append((prev.dma_res, prev.dma_val))
            return
        if prev.eng == op.eng and not same_ok:
            return
        op.deps.append(prev)
        prev.milestone = True

    def op(self, eng, fn, reads=(), writes=(), dma=None, uwrites=()):
        o = Op(eng, _freeze(fn))
        is_dma = dma is not None
        if self.pending[eng]:
            for p in self.pending[eng]:
                self._dep_on(o, p, same_ok=(eng != PE))
            self.pending[eng] = []
        for r in reads:
            self._dep_on(o, r.w, same_ok=(is_dma or eng != PE))
            for w_ in r.ws:
                self._dep_on(o, w_, same_ok=(is_dma or eng != PE))
        for r in writes:
            self._dep_on(o, r.w, same_ok=(is_dma or eng != PE))
            for w_ in r.ws:
                self._dep_on(o, w_, same_ok=(is_dma or eng != PE))
            for e, rd in r.r.items():
                self._dep_on(o, rd, same_ok=(is_dma or eng != PE))
        for r in uwrites:
            self._dep_on(o, r.w, same_ok=(is_dma or eng != PE))
            for e, rd in r.r.items():
                self._dep_on(o, rd, same_ok=(is_dma or eng != PE))
        if is_dma:
            if dma.dsem is None:
                self.dma_resources.append(dma)
                dma.dsem = True
            dma.dcnt += 1
            o.dma_res = dma
            o.dma_val = dma.dcnt * 16
            self.last_dma[id(dma)] = o
        for r in reads:
            key = eng if not is_dma else ("dma", id(o))
            r.r[key] = o
        for r in writes:
            r.w = o
            r.ws = []
            r.r = {}
        for r in uwrites:
            r.ws.append(o)
        self.streams[eng].append(o)
        return o

    def emit(self, final_waits=()):
        nc = self.nc
        with ExitStack() as es:
            sems = {e: es.enter_context(nc.semaphore("s_" + e)) for e in ENGS}
            for i, r in enumerate(self.dma_resources):
                r.dsem = es.enter_context(nc.semaphore("d%d" % i))
            for e in ENGS:
                c = 0
                for o in self.streams[e]:
                    if o.dma_res is None and o.milestone:
                        c += 1
                        o.val = c
            block = es.enter_context(nc.Block())

            def run(ename):
                def body(engine):
                    known = {}
                    dknown = {}
                    for o in self.streams[ename]:
                        for d in o.deps:
                            if known.get(d.eng, 0) < d.val:
                                engine.wait_ge(sems[d.eng], d.val)
                                known[d.eng] = d.val
                        for (r, v) in o.ddeps:
                            if dknown.get(id(r), 0) < v:
                                engine.wait_ge(r.dsem, v)
                                dknown[id(r)] = v
                        ins = o.fn(engine)
                        if o.dma_res is not None:
                            ins.then_inc(o.dma_res.dsem, 16)
                        elif o.milestone:
                            ins.then_inc(sems[ename], 1)
                    if ename == SP:
                        for r in final_waits:
                            engine.wait_ge(r.dsem, r.dcnt * 16)
                return body

            block.tensor(run(PE))
            block.scalar(run(ACT))
            block.vector(run(DVE))
            block.gpsimd(run(POOL))
            block.sync(run(SP))


class RD(dict):
    def __missing__(self, k):
        v = Res(str(k))
        self[k] = v
        return v


class _Stop(Exception):
    pass


STOP = None
SKIPB = None


def build_nc():
    nc = bass.Bass("TRN2", target_bir_lowering=False)
    S = Sched(nc)
    R = RD()
    out_res = []
    try:
        _build_body(nc, S, R, out_res)
    except _Stop:
        pass
    S.emit(final_waits=list({id(r): r for r in out_res}.values()))
    return nc


def _build_body(nc, S, R, out_res):
    def stop_at(tag):
        if STOP == tag:
            raise _Stop()

    def din(name, shape, dt=F32):
        return nc.dram_tensor(name, list(shape), dt, kind="ExternalInput").ap()

    def dout(name, shape):
        return nc.dram_tensor(name, list(shape), F32, kind="ExternalOutput").ap()

    xin = din("xin", [NT, 1024])
    s5re_in = din("s5re_in", [128, 16, 16])
    s5im_in = din("s5im_in", [128, 16, 16])
    hg_in = din("hg_in", [16, 4, 128, 128])
    w_in = din("w_in", [1024, 2560])
    w_glu = din("w_glu", [512, 512])
    w_out = din("w_out", [1024, 1024])
    w_gate = din("w_gate", [1024, 2816])
    w_up = din("w_up", [1024, 2816])
    w_down = din("w_down", [2816, 1024])
    gmix_d = din("gmix", [128, 8])
    gffn_d = din("gffn", [128, 8])
    nfin_d = din("nfin", [1024])
    hgn_d = din("hgn", [128, 1])
    lbp_d = din("lbp", [128, 8])
    are_d = din("a_re", [128, 16])
    aim_d = din("a_im", [128, 16])
    ldt_d = din("ldt", [128, 16])
    bre_d = din("b_re", [128, 16, 16])
    bim_d = din("b_im", [128, 16, 16])
    cre_d = din("c_re", [128, 16, 16])
    cim_d = din("c_im", [128, 16, 16])
    dsk_d = din("dsk", [128, 4])
    ident_d = din("ident", [128, 128], BF16)
    m64_d = din("m64", [128, 128], BF16)
    m8_d = din("m8", [128, 128], BF16)
    rmask_d = din("rmask", [128, NT])
    rowm_d = din("rowm", [128, 16])
    jrow_d = din("jrow", [128, 256])

    y_o = dout("y", [NT, 1024])
    s5p_re_o = dout("s5p_re", [128, 16])
    s5p_im_o = dout("s5p_im", [128, 16])
    hgp_o = dout("hgp", [4, 128, 128])
    s5s_re_o = dout("s5s_re", [128, 16, 16])
    s5s_im_o = dout("s5s_im", [128, 16, 16])
    hgs_o = dout("hgs", [16, 4, 128, 128])

    LIMIT = 229376
    cur = [16640]
    allocs = {}

    def sb(name, shape, dt, at=None):
        nbytes = int(np.prod(shape[1:])) * (4 if dt == F32 else 2)
        nbytes = (nbytes + 63) // 64 * 64
        if at is None:
            o = cur[0]
            cur[0] += nbytes
        else:
            o = at
        assert o + nbytes <= LIMIT, (name, o, nbytes)
        allocs[name] = (o, nbytes)
        return nc.alloc_sbuf_tensor_at(name, list(shape), dt, offset=o)

    NBANK = 8
    banks = [nc.alloc_psum_tensor("bank%d" % i, [128, 512], F32) for i in range(NBANK)]
    bank_i = [0]

    def nextbank():
        i = bank_i[0] % NBANK
        bank_i[0] += 1
        return banks[i], R[("bank", i)]

    pool_i = {}

    def poolbank(lo, n):
        k = pool_i.get((lo, n), 0)
        pool_i[(lo, n)] = k + 1
        i = lo + k % n
        return banks[i], R[("bank", i)]

    def fixbank(i):
        return banks[i], R[("bank", i)]

    alt = [0]

    def alt_eng():
        alt[0] += 1
        return ACT if alt[0] % 2 else DVE

    def copy_op(eng, out, in_, reads, writes):
        if eng == ACT:
            S.op(ACT, lambda e: e.activation(out=out, in_=in_, func=AF.Copy), reads=reads, writes=writes)
        else:
            S.op(eng, lambda e: e.tensor_copy(out=out, in_=in_), reads=reads, writes=writes)

    def load(out, in_, res, eng=SP):
        S.op(eng, lambda e: e.dma_start(out=out, in_=in_), writes=[res], dma=res)

    def store(out, in_, res):
        S.op(SP, lambda e: e.dma_start(out=out, in_=in_), reads=[res], dma=res)
        out_res.append(res)

    ident = sb("ident", [128, 128], BF16)
    m64 = sb("m64", [128, 128], BF16)
    m8 = sb("m8", [128, 128], BF16)
    onesm = sb("onesm", [128, 128], BF16)
    rmask = sb("rmask", [128, NT], F32)
    rowm = sb("rowm", [128, 16], F32)
    gmix = sb("gmix", [128, 8], F32)
    gffn = sb("gffn", [128, 8], F32)
    nfin = sb("nfin", [128, 1024], F32)
    hgn = sb("hgn", [128, 1], F32)
    lbp = sb("lbp", [128, 8], F32)
    lb = sb("lb", [128, 4], F32)
    oml = sb("oml", [128, 4], F32)
    noml = sb("noml", [128, 4], F32)
    dsk = sb("dsk", [128, 4], F32)
    epsb = sb("epsb", [128, 1], F32)
    ssb = sb("ssb", [128, 4], F32)
    Cst = R["const"]
    for t, d in ((ident, ident_d), (m64, m64_d), (m8, m8_d), (rmask, rmask_d), (rowm, rowm_d), (gmix, gmix_d),
                 (gffn, gffn_d), (hgn, hgn_d), (lbp, lbp_d), (dsk, dsk_d)):
        load(t[:], d, Cst)
    load(nfin[:], nfin_d.partition_broadcast(128), Cst)
    S.op(POOL, lambda e: e.memset(epsb[:], 1e-6), writes=[Cst])
    S.op(POOL, lambda e: e.memset(onesm[:], 1.0 / 128), writes=[Cst])
    S.op(DVE, lambda e: e.tensor_tensor(out=lb[:], in0=lbp[:, 0:4], in1=lbp[:, 4:8], op=ALU.subtract), reads=[Cst], writes=[R["lb"]])
    S.op(ACT, lambda e: e.activation(out=lb[:], in_=lb[:], func=AF.Sigmoid), reads=[R["lb"]], writes=[R["lb"]])
    S.op(DVE, lambda e: e.tensor_scalar(out=oml[:], in0=lb[:], scalar1=-1.0, scalar2=1.0, op0=ALU.mult, op1=ALU.add), reads=[R["lb"]], writes=[R["oml"]])
    S.op(DVE, lambda e: e.tensor_scalar(out=noml[:], in0=oml[:], scalar1=-1.0, scalar2=None, op0=ALU.mult), reads=[R["oml"]], writes=[R["noml"]])

    mixT = sb("mixT", [128, 8, NT], BF16)
    uP = sb("uP", [128, 4, NT], BF16)
    mark_L1 = cur[0]
    qT = sb("qT", [128, 4, NT], BF16)
    kk = sb("kk", [128, 4, NT], BF16)
    sog = sb("sog", [128, 4, NT], BF16)
    iv = sb("iv", [128, NTILE, 512], BF16)
    logf = sb("logf", [128, 4, NT], F32)
    mark_B = cur[0]
    hT = sb("hT", [128, 8, NT], BF16, at=allocs["mixT"][0])
    xbuf = [sb("xbuf%d" % i, [128, 1024], F32) for i in range(4)]
    xnb = [sb("xnb%d" % i, [128, 1024], BF16) for i in range(4)]
    NXB = [4]
    wct = [sb("wct%d" % i, [128, 8, 128], BF16) for i in range(2)]
    wiv = sb("wiv", [128, 8, 512], BF16)
    sigt = [sb("sigt%d" % i, [128, 512], F32) for i in range(2)]

    def tile_of(c0, n):
        return [R[("tok", t)] for t in range(c0 // 128, (c0 + n) // 128)]

    def norm_stats(tile, src_ap, src_res):
        b = tile % NXB[0]
        junk = xnb[b]
        ss = ssb[:, b:b + 1]
        rs = R[("ss", b)]
        rxn = R[("xn", b)]
        S.op(DVE, lambda e: e.memset(ss, 0.0), writes=[rs])
        S.op(ACT, lambda e: e.activation(out=junk[:], in_=src_ap, func=AF.Square, accum_out=ss), reads=[src_res], writes=[rxn, rs])
        S.op(ACT, lambda e: e.activation(out=ss, in_=ss, func=AF.Sqrt, scale=1.0 / 1024, bias=epsb[:, 0:1]), reads=[rs, Cst], writes=[rs])
        S.op(DVE, lambda e: e.reciprocal(out=ss, in_=ss), reads=[rs], writes=[rs])
        S.op(DVE, lambda e: e.tensor_scalar(out=junk[:], in0=src_ap, scalar1=ss, scalar2=None, op0=ALU.mult), reads=[rs, src_res], writes=[rxn])

    def norm_trans(tile, gvec, dstT, dst_key):
        b = tile % NXB[0]
        junk = xnb[b]
        rxn = R[("xn", b)]
        bk, rb = nextbank()
        pst = bk[:].bitcast(BF16)
        for c in range(8):
            S.op(PE, lambda e, c=c: e.transpose(out=pst[:, c * 128:(c + 1) * 128], in_=junk[:, c * 128:(c + 1) * 128], identity=ident[:]),
                 reads=[rxn, Cst], writes=[rb])
        o = dstT[:, :, tile * 128:(tile + 1) * 128]
        i_ = pst[:, 0:1024].rearrange("p (c n) -> p c n", c=8)
        gb = gvec[:, :].unsqueeze(2).to_broadcast([128, 8, 128])
        S.op(DVE, lambda e: e.tensor_tensor(out=o, in0=i_, in1=gb, op=ALU.mult), reads=[rb, Cst], writes=[R[(dst_key, tile)]])

    def preA(t):
        b = t % 4
        load(xbuf[b][:], xin[t * 128:(t + 1) * 128, :], R[("xbuf", b)])
        norm_stats(t, xbuf[b][:], R[("xbuf", b)])

    for t in range(3):
        preA(t)
    for t in range(NTILE):
        if t + 3 < NTILE:
            preA(t + 3)
        norm_trans(t, gmix, hT, "hT")

    stop_at("A")
    def hT_res(c0, n):
        return [R[("hT", t)] for t in range(c0 // 128, (c0 + n) // 128)]

    wi = [0]
    for ct in list(range(0, 12)) + list(range(16, 20)):
        wb = wi[0] % 2
        wi[0] += 1
        load(wct[wb][:], w_in[:, ct * 128:(ct + 1) * 128].rearrange("(c p) n -> p c n", p=128), R[("wct", wb)], eng=POOL)
        kind, h = ct // 4, ct % 4
        for (c0, n) in CHUNKS:
            bk, rb = nextbank()
            for c in range(8):
                S.op(PE, lambda e, c=c, bk=bk, wb=wb, c0=c0, n=n: e.matmul(bk[:, 0:n], lhsT=wct[wb][:, c, :], rhs=hT[:, c, c0:c0 + n], start=(c == 0), stop=(c == 7)),
                     reads=[R[("wct", wb)]] + hT_res(c0, n), writes=[rb])
            if kind == 0:
                j0, nbk = c0 // 8, n // 8
                o = uP[:, h, :].rearrange("p (i j) -> p i j", i=8)[:, :, j0:j0 + nbk]
                i_ = bk[:, 0:n].rearrange("p (j i) -> p i j", i=8)
                copy_op(DVE, o, i_, [rb], [R[("uP", h)]])
            elif kind == 1:
                copy_op(ACT, qT[:, h, c0:c0 + n], bk[:, 0:n], [rb], [R[("qT", h)]])
            elif kind == 2:
                S.op(ACT, lambda e, bk=bk, n=n, h=h, c0=c0: e.activation(out=logf[:, h, c0:c0 + n], in_=bk[:, 0:n], func=AF.Sigmoid), reads=[rb], writes=[R[("logf", h)]])
                if c0 == 2048:
                    S.op(DVE, lambda e, h=h: e.tensor_scalar(out=kk[:, h, :], in0=logf[:, h, :], scalar1=noml[:, h:h + 1], scalar2=oml[:, h:h + 1], op0=ALU.mult, op1=ALU.add),
                         reads=[R[("logf", h)], R["oml"], R["noml"]], writes=[R[("kk", h)]])
                    S.op(ACT, lambda e, h=h: e.activation(out=logf[:, h, :], in_=logf[:, h, :], func=AF.Ln, scale=oml[:, h:h + 1], bias=lb[:, h:h + 1]),
                         reads=[R[("logf", h)], R["oml"], R["lb"]], writes=[R[("logf", h)]])
            else:
                S.op(ACT, lambda e, bk=bk, n=n, h=h, c0=c0: e.activation(out=sog[:, h, c0:c0 + n], in_=bk[:, 0:n], func=AF.Silu), reads=[rb], writes=[R[("sog", h)]])
                S.op(DVE, lambda e, n=n, h=h, c0=c0: e.tensor_scalar(out=sog[:, h, c0:c0 + n], in0=sog[:, h, c0:c0 + n], scalar1=hgn[:, 0:1], scalar2=None, op0=ALU.mult), reads=[R[("sog", h)], Cst], writes=[R[("sog", h)]])
    load(wiv[:], w_in[:, 1536:2048].rearrange("(c p) n -> p c n", p=128), R["wiv"], eng=POOL)
    for t in range(NTILE):
        bk, rb = nextbank()
        for c in range(8):
            S.op(PE, lambda e, c=c, bk=bk, t=t: e.matmul(bk[:, :], lhsT=hT[:, c, t * 128:(t + 1) * 128], rhs=wiv[:, c, :], start=(c == 0), stop=(c == 7)),
                 reads=[R["wiv"], R[("hT", t)]], writes=[rb])
        copy_op(alt_eng(), iv[:, t, :], bk[:, :], [rb], [R[("iv", t)]])

    stop_at("B")
    cur[0] = mark_B
    BAR = R["barrier_B"]

    def barrier(tag, extra=()):
        S.barrier(extra)

    barrier("B", [R[("wct", 0)], R[("wct", 1)], R["wiv"], R[("xbuf", 0)], R[("xbuf", 1)], R[("xbuf", 2)], R[("xbuf", 3)], Cst])

    stop_at("D0")
    Gt = sb("Gt", [128, NT], F32)
    eG = Gt
    eGn = sb("eGn", [128, NT], F32)
    eGl = sb("eGl", [128, 4, 48], F32)
    kgTM = sb("kgTM", [128, NTILE, 512], BF16, at=allocs["mixT"][0])
    attm = [sb("attm%d" % i, [128, 128], BF16) for i in range(4)]
    Sf = [sb("Sf%d" % i, [128, 128], F32) for i in range(4)]
    Sb_ = [sb("Sb%d" % i, [128, 128], BF16) for i in range(4)]
    stmp = [sb("stmp%d" % i, [128, 128], F32) for i in range(4)]
    sqb = [sb("sqb%d" % i, [128, 128], BF16) for i in range(4)]
    osb = [sb("osb%d" % i, [128, 128], F32) for i in range(4)]
    rstd = [sb("rstd%d" % i, [128, 128], F32) for i in range(2)]
    _save = cur[0]
    cur[0] = allocs["Gt"][0]
    sinf = [sb("sinf%d" % i, [128, 4, 128], F32) for i in range(4)]
    sinb = [sb("sinb%d" % i, [128, 4, 128], BF16) for i in range(2)]
    kgm = [sb("kgm%d" % i, [128, 512], BF16) for i in range(2)]
    sout = [sb("sout%d" % i, [128, 4, 128], F32) for i in range(2)]
    assert cur[0] <= allocs["eGn"][0] + allocs["eGn"][1]
    cur[0] = _save

    for h in range(4):
        rG = R["Gt"]
        S.op(DVE, lambda e, h=h: e.tensor_tensor_scan(out=Gt[:], data0=rmask[:], data1=logf[:, h, :], initial=0.0, op0=ALU.mult, op1=ALU.add),
             reads=[Cst, R[("logf", h)]], writes=[rG, R["eG"]])
        stop_at("D1")
        S.op(ACT, lambda e: e.activation(out=eGn[:], in_=Gt[:], func=AF.Exp, scale=-1.0), reads=[rG], writes=[R["eGn"]])
        S.op(ACT, lambda e: e.activation(out=eG[:], in_=Gt[:], func=AF.Exp), reads=[rG], writes=[rG, R["eG"]])
        stop_at("D2")
        S.op(DVE, lambda e, h=h: e.tensor_tensor(out=qT[:, h, :], in0=qT[:, h, :], in1=eG[:], op=ALU.mult), reads=[R["eG"], R[("qT", h)]], writes=[R[("qT", h)]])
        S.op(DVE, lambda e, h=h: e.tensor_tensor(out=kk[:, h, :], in0=kk[:, h, :], in1=eGn[:], op=ALU.mult), reads=[R["eGn"], R[("kk", h)]], writes=[R[("kk", h)]])
        stop_at("D3")
        S.op(DVE, lambda e, h=h: e.tensor_copy(out=eGl[:, h, 0:32], in_=eG[:, 63:2048:64]), reads=[R["eG"]], writes=[R["eGl"]])
        S.op(DVE, lambda e, h=h: e.tensor_copy(out=eGl[:, h, 32:48], in_=eG[:, 2055:NT:8]), reads=[R["eG"]], writes=[R["eGl"]])
        stop_at("D4")
        for t in range(NTILE):
            if t % 4 == 0:
                bk, rb = nextbank()
                pst = bk[:].bitcast(BF16)
            S.op(PE, lambda e, pst=pst, t=t, h=h: e.transpose(out=pst[:, (t % 4) * 128:(t % 4 + 1) * 128], in_=kk[:, h, t * 128:(t + 1) * 128], identity=ident[:]),
                 reads=[R[("kk", h)], Cst], writes=[rb])
            if t % 4 == 3 or t == NTILE - 1:
                t0 = t - (t % 4)
                nt_ = t - t0 + 1
                copy_op(alt_eng(), kgTM[:, t0:t0 + nt_, h * 128:(h + 1) * 128], pst[:, 0:nt_ * 128].rearrange("p (t k) -> p t k", k=128), [rb], [R["kgTM"]])

    stop_at("Dprep")
    for h in range(4):
        S.op(POOL, lambda e, h=h: e.memset(Sf[h][:], 0.0), writes=[R[("Sf", h)]])
        S.op(POOL, lambda e, h=h: e.memset(Sb_[h][:], 0.0), writes=[R[("Sb", h)]])

    def post1(t, h):
        bo, rbo = fixbank(h)
        b, b2 = h, h % 2
        S.op(DVE, lambda e: e.tensor_copy(out=osb[b][:], in_=bo[:, 0:128]), reads=[rbo], writes=[R[("osb", b)]])
        S.op(DVE, lambda e: e.tensor_tensor(out=sqb[b][:], in0=osb[b][:], in1=osb[b][:], op=ALU.mult), reads=[R[("osb", b)]], writes=[R[("sqb", b)]])

    def post2(t, h):
        b, b2 = h, h % 2
        bm, rbm = poolbank(6, 2)
        S.op(PE, lambda e: e.matmul(bm[:, 0:128], lhsT=onesm[:], rhs=sqb[b][:], start=True, stop=True), reads=[Cst, R[("sqb", b)]], writes=[rbm])
        S.op(ACT, lambda e: e.activation(out=rstd[b2][:], in_=bm[:, 0:128], func=AF.Ln, bias=epsb[:, 0:1]), reads=[rbm, Cst], writes=[R[("rstd", b2)]])
        S.op(ACT, lambda e: e.activation(out=rstd[b2][:], in_=rstd[b2][:], func=AF.Exp, scale=-0.5), reads=[R[("rstd", b2)]], writes=[R[("rstd", b2)]])
        S.op(POOL, lambda e: e.tensor_tensor(out=osb[b][:], in0=osb[b][:], in1=rstd[b2][:], op=ALU.mult), reads=[R[("rstd", b2)], R[("osb", b)]], writes=[R[("osb", b)]])
        S.op(POOL, lambda e: e.tensor_tensor(out=mixT[:, 4 + h, t * 128:(t + 1) * 128], in0=osb[b][:], in1=sog[:, h, t * 128:(t + 1) * 128], op=ALU.mult),
             reads=[R[("osb", b)], R[("sog", h)]], writes=[R[("mix", t)]])

    def hg_post(t, h, bo=None, rbo=None):
        post1(t, h)
        post2(t, h)

    def stageA(t, h, mask):
        cols = slice(t * 128, (t + 1) * 128)
        ba, rba = poolbank(4, 2)
        S.op(PE, lambda e: e.matmul(ba[:, 0:128], lhsT=kk[:, h, cols], rhs=qT[:, h, cols], start=True, stop=True),
             reads=[R[("kk", h)], R[("qT", h)]], writes=[rba])
        S.op(DVE, lambda e: e.tensor_tensor(out=attm[h][:], in0=ba[:, 0:128], in1=mask[:], op=ALU.mult), reads=[rba, Cst], writes=[R[("attm", h)]])

    def stageB(t, h):
        bo, rbo = fixbank(h)
        S.op(PE, lambda e: e.matmul(bo[:, 0:128], lhsT=iv[:, t, h * 128:(h + 1) * 128], rhs=attm[h][:], start=True, stop=False),
             reads=[R[("iv", t)], R[("attm", h)]], writes=[rbo])

    def stageC(t, c, h):
        ci = t * 2 + c
        bo, rbo = fixbank(h)
        ccols = slice(t * 128 + c * 64, t * 128 + c * 64 + 64)
        S.op(PE, lambda e: e.matmul(bo[:, c * 64:(c + 1) * 64], lhsT=Sb_[h][:], rhs=qT[:, h, ccols], start=False, stop=(c == 1)),
             reads=[R[("Sb", h)], R[("qT", h)]], writes=[rbo])
        bkv, rbkv = poolbank(6, 2)
        S.op(PE, lambda e: e.matmul(bkv[:, 0:128], lhsT=kgTM[c * 64:(c + 1) * 64, t, h * 128:(h + 1) * 128], rhs=iv[c * 64:(c + 1) * 64, t, h * 128:(h + 1) * 128], start=True, stop=True),
             reads=[R["kgTM"], R[("iv", t)]], writes=[rbkv])
        S.op(DVE, lambda e: e.tensor_tensor(out=stmp[h][:], in0=bkv[:, 0:128], in1=Sf[h][:], op=ALU.add), reads=[rbkv, R[("Sf", h)]], writes=[R[("stmp", h)]])
        S.op(DVE, lambda e: e.tensor_scalar(out=Sb_[h][:], in0=stmp[h][:], scalar1=eGl[:, h, ci:ci + 1], scalar2=None, op0=ALU.mult), reads=[R[("stmp", h)], R["eGl"]], writes=[R[("Sb", h)]])
        S.op(ACT, lambda e: e.activation(out=Sf[h][:], in_=stmp[h][:], func=AF.Copy, scale=eGl[:, h, ci:ci + 1]), reads=[R[("stmp", h)], R["eGl"]], writes=[R[("Sf", h)]])

    for h in range(4):
        stageA(0, h, m64)
    for t in range(16):
        if t > 0:
            for h in range(4):
                post1(t - 1, h)
        for h in range(4):
            stageB(t, h)
        for h in range(4):
            stageC(t, 0, h)
        if t > 0:
            for h in range(4):
                post2(t - 1, h)
        if t + 1 < 16:
            for h in range(4):
                stageA(t + 1, h, m64)
        for h in range(4):
            stageC(t, 1, h)
    for h in range(4):
        post1(15, h)
    for h in range(4):
        post2(15, h)
    for h in range(4):
        store(hgp_o[h], Sf[h][:], R[("Sf", h)])

    stop_at("Dp")
    t = 16
    cols = slice(2048, NT)
    bos = []
    for h in range(4):
        ba, rba = poolbank(4, 2)
        S.op(PE, lambda e, ba=ba, h=h: e.matmul(ba[:, 0:128], lhsT=kk[:, h, cols], rhs=qT[:, h, cols], start=True, stop=True), reads=[R[("kk", h)], R[("qT", h)]], writes=[rba])
        ab = h
        S.op(DVE, lambda e, ba=ba, ab=ab: e.tensor_tensor(out=attm[ab][:], in0=ba[:, 0:128], in1=m8[:], op=ALU.mult), reads=[rba, Cst], writes=[R[("attm", ab)]])
        bo, rbo = fixbank(h)
        bos.append((bo, rbo))
        S.op(PE, lambda e, bo=bo, ab=ab, h=h: e.matmul(bo[:, 0:128], lhsT=iv[:, 16, h * 128:(h + 1) * 128], rhs=attm[ab][:], start=True, stop=False),
             reads=[R[("iv", 16)], R[("attm", ab)]], writes=[rbo])
    ALIAS_D = [R["Gt"], R["eG"], R["eGn"]]

    def sload(q):
        b4 = q % 4
        S.op(SP, lambda e: e.dma_start(out=sinf[b4][:], in_=hg_in[q].rearrange("h k v -> k h v")), writes=[R[("sinf", b4)]], dma=R[("sinf", b4)], uwrites=ALIAS_D)

    for q in range(3):
        sload(q)
    def sprep(q):
        b, b4 = q % 2, q % 4
        S.op(ACT, lambda e: e.activation(out=sinb[b][:], in_=sinf[b4][:], func=AF.Copy), reads=[R[("sinf", b4)]], writes=[R[("sinb", b)]], uwrites=ALIAS_D)
        S.op(DVE, lambda e: e.tensor_scalar(out=kgm[b][:], in0=kgTM[:, 16, :], scalar1=rowm[:, q:q + 1], scalar2=None, op0=ALU.mult), reads=[R["kgTM"], Cst], writes=[R[("kgm", b)]], uwrites=ALIAS_D)

    sprep(0)
    for q in range(16):
        b = q % 2
        b4 = q % 4
        if q + 3 < 16:
            sload(q + 3)
        bkvs = []
        for h in range(4):
            bo, rbo = bos[h]
            S.op(PE, lambda e, bo=bo: e.matmul(bo[:, q * 8:(q + 1) * 8], lhsT=sinb[b][:, h, :], rhs=qT[:, h, 2048 + q * 8:2048 + (q + 1) * 8], start=False, stop=(q == 15)),
                 reads=[R[("sinb", b)], R[("qT", h)]], writes=[rbo])
            bkv, rbkv = poolbank(6, 2)
            S.op(PE, lambda e, bkv=bkv: e.matmul(bkv[:, 0:128], lhsT=kgm[b][:, h * 128:(h + 1) * 128], rhs=iv[:, 16, h * 128:(h + 1) * 128], start=True, stop=True),
                 reads=[R[("kgm", b)], R[("iv", 16)]], writes=[rbkv])
            S.op(DVE, lambda e, bkv=bkv: e.tensor_tensor(out=stmp[h][:], in0=bkv[:, 0:128], in1=sinf[b4][:, h, :], op=ALU.add), reads=[rbkv, R[("sinf", b4)]], writes=[R[("stmp", h)]])
            if h == 1 and q + 1 < 16:
                sprep(q + 1)
            S.op(ACT, lambda e: e.activation(out=sout[b][:, h, :], in_=stmp[h][:], func=AF.Copy, scale=eGl[:, h, 32 + q:33 + q]), reads=[R[("stmp", h)], R["eGl"]], writes=[R[("sout", b)]], uwrites=ALIAS_D)
        store(hgs_o[q].rearrange("h k v -> k h v"), sout[b][:], R[("sout", b)])
    for h in range(4):
        hg_post(16, h, *bos[h])

    stop_at("D")
    barrier("D", [R[("sinf", 0)], R[("sinf", 1)], R[("sinf", 2)], R[("sinf", 3)], R[("sout", 0)], R[("sout", 1)]] + [R[("Sf", h_)] for h_ in range(4)])
    cur[0] = mark_L1
    a_re = sb("a_re", [128, 16], F32)
    a_im = sb("a_im", [128, 16], F32)
    ldt = sb("ldt", [128, 16], F32)
    b_re = sb("b_re", [128, 16, 16], F32)
    b_im = sb("b_im", [128, 16, 16], F32)
    c_re = sb("c_re", [128, 16, 16], F32)
    c_im = sb("c_im", [128, 16, 16], F32)
    jrow = sb("jrow", [128, 256], F32)
    hin_re = sb("hin_re", [128, 16, 16], F32)
    hin_im = sb("hin_im", [128, 16, 16], F32)
    P5 = R["p5"]
    for t_, d_ in ((a_re, are_d), (a_im, aim_d), (ldt, ldt_d), (b_re, bre_d), (b_im, bim_d), (c_re, cre_d), (c_im, cim_d), (jrow, jrow_d), (hin_re, s5re_in), (hin_im, s5im_in)):
        load(t_[:], d_, P5)
    names = ["dt", "ard", "th", "mag", "t0", "t1", "cs", "sn", "den", "nr", "fre", "fim", "rho8", "phi", "nth"]
    V = {n_: sb("v_" + n_, [128, 16], F32) for n_ in names}
    pw0 = sb("pw0", [128, 4, 16], F32)
    pw1 = sb("pw1", [128, 4, 16], F32)
    APR = sb("APR", [128, 9, 16], F32)
    API = sb("API", [128, 9, 16], F32)
    NAPR = sb("NAPR", [128, 9, 16], F32)
    NAPI = sb("NAPI", [128, 9, 16], F32)
    bb_re = sb("bb_re", [128, 16, 16], F32)
    bb_im = sb("bb_im", [128, 16, 16], F32)
    tbb = sb("tbb", [128, 16, 16], F32)

    def pv(fn, reads=(), writes=()):
        S.op(DVE, fn, reads=[P5] + list(reads), writes=[P5] + list(writes))

    def pa(fn):
        S.op(ACT, fn, reads=[P5], writes=[P5])

    def tsc(o, i, s1, s2=None, op0=ALU.mult, op1=None):
        if op1 is None:
            pv(lambda e: e.tensor_scalar(out=o, in0=i, scalar1=s1, scalar2=None, op0=op0))
        else:
            pv(lambda e: e.tensor_scalar(out=o, in0=i, scalar1=s1, scalar2=s2, op0=op0, op1=op1))

    def tt(o, a, b_, op):
        pv(lambda e: e.tensor_tensor(out=o, in0=a, in1=b_, op=op))

    def sin_of(out, ang, shift):
        tsc(V["t0"][:], ang, 1.0 / TWO_PI, shift / TWO_PI, ALU.mult, ALU.add)
        tsc(V["t0"][:], V["t0"][:], MAGIC, None, ALU.add)
        tsc(V["t0"][:], V["t0"][:], -MAGIC, None, ALU.add)
        pv(lambda e: e.scalar_tensor_tensor(out=V["t1"][:], in0=V["t0"][:], scalar=-TWO_PI, in1=ang, op0=ALU.mult, op1=ALU.add))
        tsc(V["t1"][:], V["t1"][:], -PI_SAFE - shift, PI_SAFE - shift, ALU.max, ALU.min)
        pa(lambda e: e.activation(out=out, in_=V["t1"][:], func=AF.Sin, bias=shiftb[shift]))

    cq = sb("cq", [128, 1], F32)
    cm = sb("cm", [128, 1], F32)
    S.op(DVE, lambda e: e.memset(cq[:], 0.25), writes=[P5])
    S.op(DVE, lambda e: e.memset(cm[:], MAGIC), writes=[P5])
    ncm = sb("ncm", [128, 1], F32)
    S.op(DVE, lambda e: e.memset(ncm[:], -MAGIC), writes=[P5])
    shb0 = sb("shb0", [128, 1], F32)
    shb1 = sb("shb1", [128, 1], F32)
    S.op(DVE, lambda e: e.memset(shb0[:], 0.0), writes=[P5])
    S.op(DVE, lambda e: e.memset(shb1[:], math.pi / 2), writes=[P5])
    shiftb = {0.0: shb0[:, 0:1], math.pi / 2: shb1[:, 0:1]}

    pa(lambda e: e.activation(out=V["dt"][:], in_=ldt[:], func=AF.Exp))
    tt(V["ard"][:], a_re[:], V["dt"][:], ALU.mult)
    tt(V["th"][:], a_im[:], V["dt"][:], ALU.mult)
    pa(lambda e: e.activation(out=V["mag"][:], in_=V["ard"][:], func=AF.Exp))
    pa(lambda e: e.activation(out=V["rho8"][:], in_=V["ard"][:], func=AF.Exp, scale=8.0))
    sin_of(V["sn"][:], V["th"][:], 0.0)
    sin_of(V["cs"][:], V["th"][:], math.pi / 2)
    tsc(V["phi"][:], V["th"][:], 8.0 / TWO_PI)
    pv(lambda e: e.memset(APR[:, 0, :], 1.0))
    pv(lambda e: e.memset(API[:, 0, :], 0.0))
    tt(APR[:, 1, :], V["mag"][:], V["cs"][:], ALU.mult)
    tt(API[:, 1, :], V["mag"][:], V["sn"][:], ALU.mult)
    for n_ in (1, 2, 4):
        lo, hi = n_ + 1, 2 * n_ + 1
        bre_ = APR[:, n_:n_ + 1, :].to_broadcast([128, n_, 16])
        bim_ = API[:, n_:n_ + 1, :].to_broadcast([128, n_, 16])
        t0_, t1_ = pw0[:, 0:n_, :], pw1[:, 0:n_, :]
        tt(t0_, APR[:, 1:n_ + 1, :], bre_, ALU.mult)
        tt(t1_, API[:, 1:n_ + 1, :], bim_, ALU.mult)
        tt(APR[:, lo:hi, :], t0_, t1_, ALU.subtract)
        tt(t0_, APR[:, 1:n_ + 1, :], bim_, ALU.mult)
        tt(t1_, API[:, 1:n_ + 1, :], bre_, ALU.mult)
        tt(API[:, lo:hi, :], t0_, t1_, ALU.add)
    tsc(NAPR[:], APR[:], -1.0)
    tsc(NAPI[:], API[:], -1.0)
    tt(V["t0"][:], a_re[:], a_re[:], ALU.mult)
    tt(V["t1"][:], a_im[:], a_im[:], ALU.mult)
    tt(V["den"][:], V["t0"][:], V["t1"][:], ALU.add)
    pv(lambda e: e.reciprocal(out=V["den"][:], in_=V["den"][:]))
    tsc(V["nr"][:], APR[:, 1, :], -1.0, None, ALU.add)
    tt(V["t0"][:], V["nr"][:], a_re[:], ALU.mult)
    tt(V["t1"][:], API[:, 1, :], a_im[:], ALU.mult)
    tt(V["t0"][:], V["t0"][:], V["t1"][:], ALU.add)
    tt(V["fre"][:], V["t0"][:], V["den"][:], ALU.mult)
    tt(V["t0"][:], API[:, 1, :], a_re[:], ALU.mult)
    tt(V["t1"][:], V["nr"][:], a_im[:], ALU.mult)
    tt(V["t0"][:], V["t0"][:], V["t1"][:], ALU.subtract)
    tt(V["fim"][:], V["t0"][:], V["den"][:], ALU.mult)
    bc16 = lambda v: v.unsqueeze(2).to_broadcast([128, 16, 16])
    tt(bb_re[:], b_re[:], bc16(V["fre"][:]), ALU.mult)
    tt(tbb[:], b_im[:], bc16(V["fim"][:]), ALU.mult)
    tt(bb_re[:], bb_re[:], tbb[:], ALU.subtract)
    tt(bb_im[:], b_im[:], bc16(V["fre"][:]), ALU.mult)
    tt(tbb[:], b_re[:], bc16(V["fim"][:]), ALU.mult)
    tt(bb_im[:], bb_im[:], tbb[:], ALU.add)

    stop_at("Cparam")
    CALall = sb("CALall", [128, 8, 9, 2, 128], BF16)
    ZZall = sb("ZZall", [128, 4, 8, 2, 128], BF16)
    CAL = [CALall[:, i] for i in range(8)]
    ZZ = [ZZall[:, i] for i in range(4)]
    WBL = [sb("WBL%d" % i, [128, 8, 2, 128], BF16) for i in range(4)]
    tca = sb("tca", [128, 4, 9, 16], F32)
    S.op(POOL, lambda e: e.memset(CALall[:], 0.0), writes=[R[("CAL", i)] for i in range(8)])
    S.op(POOL, lambda e: e.memset(ZZall[:], 0.0), writes=[R[("ZZ", i)] for i in range(4)])
    Kblk = sb("Kblk", [128, 8, 128], BF16)
    Hp_re = sb("Hp_re", [128, 8, NB], BF16)
    Hp_im = sb("Hp_im", [128, 8, NB], BF16)
    cosT = [sb("cosT%d" % i, [128, 256], F32) for i in range(2)]
    sinT = [sb("sinT%d" % i, [128, 256], F32) for i in range(2)]
    ang = sb("ang", [128, 256], F32)
    ang2 = [sb("ang2_%d" % i, [128, 256], F32) for i in range(2)]
    rhoR = sb("rhoR", [128, 256], F32)
    gin_re = sb("gin_re", [128, 256], F32)
    gin_im = sb("gin_im", [128, 256], F32)
    g_re = [sb("g_re%d" % i, [128, 256], F32) for i in range(2)]
    g_im = [sb("g_im%d" % i, [128, 256], F32) for i in range(2)]
    tA = sb("tA", [128, 256], F32)
    tB = sb("tB", [128, 256], F32)
    pA = sb("pA", [128, 256], F32)
    pB = sb("pB", [128, 256], F32)
    hf_re = sb("hf_re", [128, 256], F32)
    hf_im = sb("hf_im", [128, 256], F32)
    hfin_re = sb("hfin_re", [128, 16], F32)
    hfin_im = sb("hfin_im", [128, 16], F32)
    hnew_re = sb("hnew_re", [128, 16, 16], F32)
    hnew_im = sb("hnew_im", [128, 16, 16], F32)
    ts1 = sb("ts1", [128, 16], F32)
    gbuf = sb("gbuf", [128, 4, NT], BF16)
    ytmp = [sb("ytmp%d" % i, [128, NB], F32) for i in range(2)]
    wglu = sb("wglu", [128, 4, 512], BF16)
    sgm = [sb("sgm%d" % i, [128, 512], BF16) for i in range(2)]
    load(wglu[:], w_glu.rearrange("(c p) n -> p c n", p=128), R["wglu"], eng=POOL)
    S.op(POOL, lambda e: e.memset(Hp_re[:, :, 0:1], 0.0), writes=[R["Hp0"]])
    S.op(POOL, lambda e: e.memset(Hp_im[:, :, 0:1], 0.0), writes=[R["Hp0"]])

    def bc_p(v, n_):
        return v.unsqueeze(2).to_broadcast([64, n_, 16])

    def bc_c(v, n_):
        return v.unsqueeze(1).to_broadcast([64, n_, 16])

    def s5_poolbuild(T, parts=(0, 1, 2, 3), eng=POOL):
        sbase = (T % 2) * 4
        rcals = [R[("CAL", sbase + b_)] for b_ in range(4)]
        rzzs = [R[("ZZ", b_)] for b_ in range(4)]
        CROW, ZROW = 8 * 9 * 2 * 128, 4 * 8 * 2 * 128
        for g2 in range(2):
            ps_ = slice(64 * g2, 64 * g2 + 64)
            rt = R[("tca", g2)]

            def coef(tab, npow):
                return bass.AP(tensor=tab, offset=64 * g2 * 144 + T * 4, ap=[[144, 64], [1, 4], [16, npow], [0, 16]])

            def cal_out(ri):
                return bass.AP(tensor=CALall, offset=64 * g2 * CROW + sbase * 2304 + ri * 128 + g2 * 16, ap=[[CROW, 64], [2336, 4], [256, 9], [1, 16]])

            def zz_out(ri):
                return bass.AP(tensor=ZZall, offset=64 * g2 * ZROW + ri * 128 + g2 * 16, ap=[[ZROW, 64], [2080, 4], [256, 8], [1, 16]])

            def bsrc(x, npow):
                return x[ps_, T * 4:(T + 1) * 4, :].unsqueeze(2).to_broadcast([64, 4, npow, 16])

            for (ri, s_a, s_b) in (((0, APR, NAPI), (1, NAPI, NAPR)) if (g2 * 2) in parts else ()):
                o_t, o_c = tca[ps_, :, :, :], cal_out(ri)
                i0a, i1a, i0b, i1b = bsrc(c_re, 9), coef(s_a, 9), bsrc(c_im, 9), coef(s_b, 9)
                S.op(eng, lambda e: e.tensor_tensor(out=o_t, in0=i0a, in1=i1a, op=ALU.mult), reads=[P5], writes=[rt])
                S.op(eng, lambda e: e.tensor_tensor(out=o_c, in0=i0b, in1=i1b, op=ALU.mult), reads=[P5], writes=rcals)
                S.op(eng, lambda e: e.tensor_tensor(out=o_c, in0=o_c, in1=o_t, op=ALU.add), reads=[rt] + rcals, writes=rcals)
            for (ri, x_a, s_a, x_b, s_b) in (((0, bb_re, APR, bb_im, NAPI), (1, bb_im, APR, bb_re, API)) if (g2 * 2 + 1) in parts else ()):
                o_t, o_z = tca[ps_, :, 0:8, :], zz_out(ri)
                i0a, i1a, i0b, i1b = bsrc(x_a, 8), coef(s_a, 8), bsrc(x_b, 8), coef(s_b, 8)
                S.op(eng, lambda e: e.tensor_tensor(out=o_t, in0=i0a, in1=i1a, op=ALU.mult), reads=[P5], writes=[rt])
                S.op(eng, lambda e: e.tensor_tensor(out=o_z, in0=i0b, in1=i1b, op=ALU.mult), reads=[P5], writes=rzzs)
                S.op(eng, lambda e: e.tensor_tensor(out=o_z, in0=o_z, in1=o_t, op=ALU.add), reads=[rt] + rzzs, writes=rzzs)

    def s5_wbl(T):
        for b in range(4):
            rzz, rwbl = R[("ZZ", b)], R[("WBL", b)]
            for half in range(4):
                bk, rb = nextbank()
                pst = bk[:].bitcast(BF16)
                for k4 in range(4):
                    idx = half * 4 + k4
                    p, ri = idx // 2, idx % 2
                    S.op(PE, lambda e, pst=pst, k4=k4, p=p, ri=ri, b=b: e.transpose(out=pst[:, k4 * 128:(k4 + 1) * 128], in_=ZZ[b][:, p, ri, :], identity=ident[:]),
                         reads=[rzz, Cst], writes=[rb])
                copy_op(alt_eng(), WBL[b][:, half * 2:half * 2 + 2, :, :].rearrange("p a r n -> p (a r n)"), pst[:, 0:512], [rb], [rwbl])

    def s5_taps(T):
        kb = []
        for hf in range(2):
            bk, rb = nextbank()
            kb.append((bk, rb))
            n_mm = 0
            for b in range(4):
                for ri in range(2):
                    S.op(PE, lambda e, bk=bk, b=b, ri=ri, hf=hf, n_mm=n_mm: e.matmul(bk[:, :], lhsT=ZZ[b][:, 0, ri, :], rhs=CAL[(T % 2) * 4 + b][:, hf * 4:hf * 4 + 4, ri, :], start=(n_mm == 0), stop=(n_mm == 7)),
                         reads=[R[("ZZ", b)], R[("CAL", (T % 2) * 4 + b)]], writes=[rb])
                    n_mm += 1
            copy_op(alt_eng(), Kblk[:, hf * 4:hf * 4 + 4, :].rearrange("p a n -> p (a n)"), bk[:, :], [rb], [R["Kblk"]])

    def s5_tables(gg):
        pb = gg % 2
        Rang, Rtab = R["ang"], R[("tab", pb)]
        cT, sT = cosT[pb], sinT[pb]
        bc256 = lambda v: v.to_broadcast([128, 256])
        a2s, a2c = ang2[0], ang2[1]
        Rs, Rc = R[("ang2", 0)], R[("ang2", 1)]
        S.op(DVE, lambda e: e.tensor_tensor(out=ang[:], in0=jrow[:], in1=bc256(V["phi"][:, gg:gg + 1]), op=ALU.mult), reads=[P5], writes=[Rang])
        S.op(ACT, lambda e: e.activation(out=a2s[:], in_=ang[:], func=AF.Identity, bias=cm[:, 0:1]), reads=[Rang, P5], writes=[Rs])
        S.op(ACT, lambda e: e.activation(out=a2s[:], in_=a2s[:], func=AF.Identity, bias=ncm[:, 0:1]), reads=[Rs, P5], writes=[Rs])
        S.op(ACT, lambda e: e.activation(out=a2c[:], in_=ang[:], func=AF.Identity, bias=cq[:, 0:1]), reads=[Rang, P5], writes=[Rc])
        S.op(ACT, lambda e: e.activation(out=a2c[:], in_=a2c[:], func=AF.Identity, bias=cm[:, 0:1]), reads=[Rc, P5], writes=[Rc])
        S.op(ACT, lambda e: e.activation(out=a2c[:], in_=a2c[:], func=AF.Identity, bias=ncm[:, 0:1]), reads=[Rc, P5], writes=[Rc])
        S.op(DVE, lambda e: e.tensor_tensor(out=a2s[:], in0=ang[:], in1=a2s[:], op=ALU.subtract), reads=[Rs, Rang], writes=[Rs])
        S.op(DVE, lambda e: e.tensor_tensor(out=a2c[:], in0=ang[:], in1=a2c[:], op=ALU.subtract), reads=[Rc, Rang], writes=[Rc])
        S.op(ACT, lambda e: e.activation(out=sT[:], in_=a2s[:], func=AF.Sin, bias=shiftb[0.0], scale=6.283185), reads=[Rs, P5], writes=[Rtab])
        S.op(ACT, lambda e: e.activation(out=cT[:], in_=a2c[:], func=AF.Sin, bias=shiftb[math.pi / 2], scale=6.283185), reads=[Rc, P5], writes=[Rtab])

    def s5_scan(T, build_next=False):
        if T == 0:
            s5_tables(0)
        for b in range(4):
            gg = T * 4 + b
            pb = gg % 2
            slot = (T % 2) * 4 + b
            rwbl = R[("WBL", b)]
            bS = []
            for ri in range(2):
                bk, rb = poolbank(0, 4)
                bS.append((bk, rb))
                for s_ in range(8):
                    S.op(PE, lambda e, bk=bk, ri=ri, s_=s_: e.matmul(bk[:, 0:NB], lhsT=WBL[b][:, 7 - s_, ri, :], rhs=uP[:, T, s_ * NB:(s_ + 1) * NB], start=(s_ == 0), stop=(s_ == 7)),
                         reads=[rwbl, R[("uP", T)]], writes=[rb])
            (bre, rbre), (bim, rbim) = bS
            Rang, Rtab, RtAB, Rgin, Rg, Rrho, RpAB, Rhf, Rts = R["ang"], R[("tab", pb)], R["tAB"], R["gin"], R[("g", pb)], R["rho"], R["pAB"], R["hf"], R["ts"]
            cT, sT, gr, gi = cosT[pb], sinT[pb], g_re[pb], g_im[pb]
            bc256 = lambda v: v.to_broadcast([128, 256])
            if T * 4 + b + 1 < 16:
                s5_tables(T * 4 + b + 1)
            S.op(DVE, lambda e: e.tensor_tensor(out=tA[:], in0=bre[:, 0:256], in1=cT[:], op=ALU.mult), reads=[rbre, Rtab], writes=[RtAB])
            S.op(DVE, lambda e: e.tensor_tensor(out=tB[:], in0=bim[:, 0:256], in1=sT[:], op=ALU.mult), reads=[rbim, Rtab], writes=[RtAB])
            S.op(DVE, lambda e: e.tensor_tensor(out=gin_re[:], in0=tA[:], in1=tB[:], op=ALU.add), reads=[RtAB], writes=[Rgin])
            S.op(DVE, lambda e: e.tensor_tensor(out=tA[:], in0=bim[:, 0:256], in1=cT[:], op=ALU.mult), reads=[rbim, Rtab, RtAB], writes=[RtAB])
            S.op(DVE, lambda e: e.tensor_tensor(out=tB[:], in0=bre[:, 0:256], in1=sT[:], op=ALU.mult), reads=[rbre, Rtab], writes=[RtAB])
            S.op(DVE, lambda e: e.tensor_tensor(out=gin_im[:], in0=tA[:], in1=tB[:], op=ALU.subtract), reads=[RtAB], writes=[Rgin])
            S.op(DVE, lambda e: e.tensor_tensor_scan(out=gr[:], data0=bc256(V["rho8"][:, gg:gg + 1]), data1=gin_re[:], initial=0.0, op0=ALU.mult, op1=ALU.add), reads=[P5, Rgin], writes=[Rg])
            S.op(DVE, lambda e: e.tensor_tensor_scan(out=gi[:], data0=bc256(V["rho8"][:, gg:gg + 1]), data1=gin_im[:], initial=0.0, op0=ALU.mult, op1=ALU.add), reads=[P5, Rgin], writes=[Rg])
            S.op(POOL, lambda e: e.tensor_tensor(out=pA[:], in0=gr[:], in1=cT[:], op=ALU.mult), reads=[Rg, Rtab], writes=[RpAB])
            S.op(POOL, lambda e: e.tensor_tensor(out=pB[:], in0=gi[:], in1=sT[:], op=ALU.mult), reads=[Rg, Rtab], writes=[RpAB])
            S.op(POOL, lambda e: e.tensor_tensor(out=hf_re[:], in0=pA[:], in1=pB[:], op=ALU.subtract), reads=[RpAB], writes=[Rhf])
            S.op(DVE, lambda e: e.tensor_tensor(out=tA[:], in0=gr[:], in1=sT[:], op=ALU.mult), reads=[Rg, Rtab, RtAB], writes=[RtAB])
            S.op(DVE, lambda e: e.tensor_tensor(out=tB[:], in0=gi[:], in1=cT[:], op=ALU.mult), reads=[Rg, Rtab], writes=[RtAB])
            S.op(DVE, lambda e: e.tensor_tensor(out=hf_im[:], in0=tA[:], in1=tB[:], op=ALU.add), reads=[RtAB], writes=[R["hfi"]])
            if build_next:
                s5_poolbuild(T + 1, parts=(b,))
            rHp = R[("Hp", slot)]
            S.op(ACT, lambda e: e.activation(out=Hp_re[:, slot, 1:256], in_=hf_re[:, 0:255], func=AF.Copy), reads=[Rhf, R["Hp0"]], writes=[rHp])
            S.op(ACT, lambda e: e.activation(out=Hp_im[:, slot, 1:256], in_=hf_im[:, 0:255], func=AF.Copy), reads=[R["hfi"]], writes=[rHp])
            S.op(ACT, lambda e: e.activation(out=hfin_re[:, gg:gg + 1], in_=hf_re[:, 255:256], func=AF.Copy), reads=[Rhf], writes=[R["hfin"]])
            S.op(ACT, lambda e: e.activation(out=hfin_im[:, gg:gg + 1], in_=hf_im[:, 255:256], func=AF.Copy), reads=[R["hfi"]], writes=[R["hfin"]])
            S.op(ACT, lambda e: e.activation(out=Hp_re[:, slot, 256:NB], in_=hin_re[:, gg, :], func=AF.Copy), reads=[P5], writes=[rHp])
            S.op(ACT, lambda e: e.activation(out=Hp_im[:, slot, 256:NB], in_=hin_im[:, gg, :], func=AF.Copy), reads=[P5], writes=[rHp])
            S.op(DVE, lambda e: e.scalar_tensor_tensor(out=ts1[:], in0=hin_im[:, gg, :], scalar=NAPI[:, 8, gg:gg + 1], in1=bre[:, 256:NB], op0=ALU.mult, op1=ALU.add), reads=[P5, rbre], writes=[Rts])
            S.op(DVE, lambda e: e.scalar_tensor_tensor(out=hnew_re[:, gg, :], in0=hin_re[:, gg, :], scalar=APR[:, 8, gg:gg + 1], in1=ts1[:], op0=ALU.mult, op1=ALU.add), reads=[P5, Rts], writes=[R["hnew"]])
            S.op(DVE, lambda e: e.scalar_tensor_tensor(out=ts1[:], in0=hin_re[:, gg, :], scalar=API[:, 8, gg:gg + 1], in1=bim[:, 256:NB], op0=ALU.mult, op1=ALU.add), reads=[P5, rbim, Rts], writes=[Rts])
            S.op(DVE, lambda e: e.scalar_tensor_tensor(out=hnew_im[:, gg, :], in0=hin_im[:, gg, :], scalar=APR[:, 8, gg:gg + 1], in1=ts1[:], op0=ALU.mult, op1=ALU.add), reads=[P5, Rts], writes=[R["hnew"]])

    def s5_conv(T):
        for i in range(8):
            bk, rb = nextbank()
            mm = []
            for tau in range(i + 1):
                mm.append((Kblk[:, tau, :], uP[:, T, (i - tau) * NB:(i - tau + 1) * NB], [R["Kblk"], R[("uP", T)]]))
            for b in range(4):
                gg = T * 4 + b
                mm.append((CAL[(T % 2) * 4 + b][:, i + 1, 0, :], Hp_re[:, (T % 2) * 4 + b, :], [R[("CAL", (T % 2) * 4 + b)], R[("Hp", (T % 2) * 4 + b)], R["Hp0"]]))
                mm.append((CAL[(T % 2) * 4 + b][:, i + 1, 1, :], Hp_im[:, (T % 2) * 4 + b, :], [R[("CAL", (T % 2) * 4 + b)], R[("Hp", (T % 2) * 4 + b)], R["Hp0"]]))
            for k_, (l_, r_, rd_) in enumerate(mm):
                S.op(PE, lambda e, bk=bk, l_=l_, r_=r_, k_=k_, last=len(mm) - 1: e.matmul(bk[:, 0:NB], lhsT=l_, rhs=r_, start=(k_ == 0), stop=(k_ == last)), reads=rd_, writes=[rb])
            yb = i % 2
            S.op(DVE, lambda e, bk=bk, yb=yb, i=i, T=T: e.scalar_tensor_tensor(out=ytmp[yb][:], in0=uP[:, T, i * NB:(i + 1) * NB], scalar=dsk[:, T:T + 1], in1=bk[:, 0:NB], op0=ALU.mult, op1=ALU.add),
                 reads=[rb, R[("uP", T)], Cst], writes=[R[("ytmp", yb)]])
            S.op(ACT, lambda e, yb=yb, i=i, T=T: e.activation(out=gbuf[:, T, :].rearrange("p (j i) -> p i j", i=8)[:, i, :], in_=ytmp[yb][:], func=AF.Gelu),
                 reads=[R[("ytmp", yb)]], writes=[R[("gbuf", T)]])

    s5_poolbuild(0, parts=(0, 1), eng=POOL)
    s5_poolbuild(0, parts=(2, 3), eng=DVE)
    s5_wbl(0)
    s5_taps(0)
    for T in range(4):
        s5_scan(T, build_next=(T + 1 < 4))
        if T + 1 < 4:
            s5_wbl(T + 1)
        s5_conv(T)
        if T + 1 < 4:
            s5_taps(T + 1)
    store(s5p_re_o, hfin_re[:], R["hfin"])
    store(s5p_im_o, hfin_im[:], R["hfin"])
    store(s5s_re_o, hnew_re[:], R["hnew"])
    store(s5s_im_o, hnew_im[:], R["hnew"])
    stop_at("C")
    for ct in range(4):
        for (c0, n) in CHUNKS:
            bk, rb = nextbank()
            for kc in range(4):
                S.op(PE, lambda e, bk=bk, kc=kc, ct=ct, c0=c0, n=n: e.matmul(bk[:, 0:n], lhsT=wglu[:, kc, ct * 128:(ct + 1) * 128], rhs=gbuf[:, kc, c0:c0 + n], start=(kc == 0), stop=(kc == 3)),
                     reads=[R["wglu"]] + [R[("gbuf", k)] for k in range(4)], writes=[rb])
            sb_i = (c0 // 512) % 2
            S.op(ACT, lambda e, bk=bk, sb_i=sb_i, n=n: e.activation(out=sgm[sb_i][:, 0:n], in_=bk[:, 0:n], func=AF.Sigmoid), reads=[rb], writes=[R[("sgm", sb_i)]])
            S.op(DVE, lambda e, sb_i=sb_i, ct=ct, c0=c0, n=n: e.tensor_tensor(out=mixT[:, ct, c0:c0 + n], in0=gbuf[:, ct, c0:c0 + n], in1=sgm[sb_i][:, 0:n], op=ALU.mult),
                 reads=[R[("sgm", sb_i)], R[("gbuf", ct)]], writes=[R[("mix", t_)] for t_ in range(c0 // 128, (c0 + n) // 128)])

    stop_at("Cglu")
    barrier("C", [P5, R["wglu"], R["hfin"], R["hnew"]])
    cur[0] = allocs["uP"][0]
    x1 = sb("x1", [128, NTILE, 1024], F32)
    actT = sb("actT", [128, 11, NT], BF16)
    wout = sb("wout", [128, 8, 1024], BF16, at=allocs["actT"][0])
    wd = sb("wd", [128, 11, 1024], BF16)
    load(wout[:], w_out.rearrange("(c p) n -> p c n", p=128), R["wout"], eng=POOL)
    xnb2 = [sb("xnb2_%d" % i, [128, 1024], BF16) for i in range(2)]
    xnb[0], xnb[1] = xnb2[0], xnb2[1]
    NXB[0] = 2
    ostg = [sb("ostg%d" % i, [128, 1024], F32) for i in range(2)]
    wg = [sb("wg%d" % i, [128, 8, 128], BF16) for i in range(2)]
    wu = [sb("wu%d" % i, [128, 8, 128], BF16) for i in range(2)]
    sgt = [sb("sgt%d" % i, [128, 512], BF16) for i in range(2)]
    def mmaddE(t):
        load(x1[:, t, :], xin[t * 128:(t + 1) * 128, :], R[("x1", t)])
        for hf in range(2):
            bk, rb = nextbank()
            for kc in range(8):
                S.op(PE, lambda e, bk=bk, kc=kc, t=t, hf=hf: e.matmul(bk[:, :], lhsT=mixT[:, kc, t * 128:(t + 1) * 128], rhs=wout[:, kc, hf * 512:(hf + 1) * 512], start=(kc == 0), stop=(kc == 7)),
                     reads=[R[("mix", t)], R["wout"]], writes=[rb])
            S.op(DVE, lambda e, bk=bk, t=t, hf=hf: e.tensor_tensor(out=x1[:, t, hf * 512:(hf + 1) * 512], in0=bk[:, :], in1=x1[:, t, hf * 512:(hf + 1) * 512], op=ALU.add),
                 reads=[rb, R[("x1", t)]], writes=[R[("x1", t)]])

    mmaddE(0)
    norm_stats(0, x1[:, 0, :], R[("x1", 0)])
    for t in range(NTILE):
        if t + 1 < NTILE:
            mmaddE(t + 1)
        norm_trans(t, gffn, mixT, "mix")
        if t + 1 < NTILE:
            norm_stats(t + 1, x1[:, t + 1, :], R[("x1", t + 1)])

    stop_at("E")
    for half in range(2):
        load(wd[:], w_down[half * 1408:(half + 1) * 1408, :].rearrange("(c p) n -> p c n", p=128), R["wd"], eng=POOL)
        for f in range(11):
            ft = half * 11 + f
            wb = ft % 2
            load(wg[wb][:], w_gate[:, ft * 128:(ft + 1) * 128].rearrange("(c p) n -> p c n", p=128), R[("wg", wb)], eng=POOL)
            load(wu[wb][:], w_up[:, ft * 128:(ft + 1) * 128].rearrange("(c p) n -> p c n", p=128), R[("wu", wb)], eng=POOL)
            for (c0, n) in CHUNKS:
                mres = [R[("mix", t_)] for t_ in range(c0 // 128, (c0 + n) // 128)]
                bg, rbg = nextbank()
                for kc in range(8):
                    S.op(PE, lambda e, bg=bg, kc=kc, wb=wb, c0=c0, n=n: e.matmul(bg[:, 0:n], lhsT=wg[wb][:, kc, :], rhs=mixT[:, kc, c0:c0 + n], start=(kc == 0), stop=(kc == 7)),
                         reads=[R[("wg", wb)]] + mres, writes=[rbg])
                bu, rbu = nextbank()
                for kc in range(8):
                    S.op(PE, lambda e, bu=bu, kc=kc, wb=wb, c0=c0, n=n: e.matmul(bu[:, 0:n], lhsT=wu[wb][:, kc, :], rhs=mixT[:, kc, c0:c0 + n], start=(kc == 0), stop=(kc == 7)),
                         reads=[R[("wu", wb)]] + mres, writes=[rbu])
                sb_i = (c0 // 512) % 2
                S.op(ACT, lambda e, bg=bg, sb_i=sb_i, n=n: e.activation(out=sgt[sb_i][:, 0:n], in_=bg[:, 0:n], func=AF.Silu), reads=[rbg], writes=[R[("sgt", sb_i)]])
                S.op(DVE, lambda e, bu=bu, sb_i=sb_i, f=f, c0=c0, n=n: e.tensor_tensor(out=actT[:, f, c0:c0 + n], in0=bu[:, 0:n], in1=sgt[sb_i][:, 0:n], op=ALU.mult),
                     reads=[rbu, R[("sgt", sb_i)]], writes=[R["wout"]] + [R[("actT", t_)] for t_ in range(c0 // 128, (c0 + n) // 128)])
        for t in range(NTILE):
            for hf in range(2):
                bk, rb = nextbank()
                for f in range(11):
                    S.op(PE, lambda e, bk=bk, f=f, t=t, hf=hf: e.matmul(bk[:, :], lhsT=actT[:, f, t * 128:(t + 1) * 128], rhs=wd[:, f, hf * 512:(hf + 1) * 512], start=(f == 0), stop=(f == 10)),
                         reads=[R[("actT", t)], R["wd"]], writes=[rb])
                S.op(DVE, lambda e, bk=bk, t=t, hf=hf: e.tensor_tensor(out=x1[:, t, hf * 512:(hf + 1) * 512], in0=bk[:, :], in1=x1[:, t, hf * 512:(hf + 1) * 512], op=ALU.add),
                     reads=[rb, R[("x1", t)]], writes=[R[("x1", t)]])
            if half == 1:
                b = t % 2
                ss = ssb[:, 2 + b:3 + b]
                rs = R[("ssf", b)]
                S.op(DVE, lambda e, ss=ss: e.memset(ss, 0.0), writes=[rs])
                S.op(ACT, lambda e, b=b, t=t, ss=ss: e.activation(out=ostg[b][:], in_=x1[:, t, :], func=AF.Square, accum_out=ss), reads=[R[("x1", t)]], writes=[R[("ostg", b)], rs])
                S.op(ACT, lambda e, ss=ss: e.activation(out=ss, in_=ss, func=AF.Sqrt, scale=1.0 / 1024, bias=epsb[:, 0:1]), reads=[rs, Cst], writes=[rs])
                S.op(DVE, lambda e, ss=ss: e.reciprocal(out=ss, in_=ss), reads=[rs], writes=[rs])
                S.op(DVE, lambda e, b=b, t=t, ss=ss: e.scalar_tensor_tensor(out=ostg[b][:], in0=x1[:, t, :], scalar=ss, in1=nfin[:], op0=ALU.mult, op1=ALU.mult),
                     reads=[rs, R[("x1", t)], Cst], writes=[R[("ostg", b)]])
                store(y_o[t * 128:(t + 1) * 128, :], ostg[b][:], R[("ostg", b)])


_NC_CACHE = {}


def _consts():
    bf = ml_dtypes.bfloat16
    s = np.arange(128)[:, None]
    t = np.arange(128)[None, :]
    m64 = ((s <= t) & (s // 64 == t // 64)).astype(bf)
    m8 = ((s <= t) & (s // 8 == t // 8)).astype(bf)
    rmask = np.ones((128, NT), np.float32)
    rmask[:, 0:2048:64] = 0.0
    rmask[:, 2048::8] = 0.0
    rowm = (np.arange(128)[:, None] // 8 == np.arange(16)[None, :]).astype(np.float32)
    jrow = np.broadcast_to(np.arange(256, dtype=np.float32)[None, :], (128, 256)).copy()
    return dict(ident=np.eye(128).astype(bf), m64=m64, m8=m8, rmask=rmask, rowm=rowm, jrow=jrow)


def _pair(a):
    sh = a.shape
    a = a.reshape(16, 2, 64, *sh[2:])
    a = np.moveaxis(a, 0, 2)
    return np.ascontiguousarray(a.reshape(128, 16, *sh[2:]))


def _unpair(a):
    sh = a.shape
    a = a.reshape(2, 64, 16, *sh[2:])
    a = np.moveaxis(a, 2, 0)
    return np.ascontiguousarray(a.reshape(32, 64, *sh[2:]))


def kernel(x_prompt, x_sample, state_s5_re, state_s5_im, state_hgrn, lb_param, norm_mix, w_in,
           s5_a_re, s5_a_im, s5_log_dt, s5_b_re, s5_b_im, s5_c_re, s5_c_im, s5_d, s5_w_glu,
           hg_norm, w_out, norm_ffn, w_gate, w_up, w_down, norm_final):
    f = lambda a: np.ascontiguousarray(np.asarray(a, dtype=np.float32))
    if "nc" not in _NC_CACHE:
        _NC_CACHE["nc"] = build_nc()
    nc = _NC_CACHE["nc"]
    cst = _consts()
    x_prompt, x_sample = f(x_prompt), f(x_sample)
    st_re, st_im, st_hg = f(state_s5_re)[0], f(state_s5_im)[0], f(state_hgrn)[0]
    shared = dict(
        w_in=f(w_in)[0], w_glu=f(s5_w_glu)[0], w_out=f(w_out)[0], w_gate=f(w_gate)[0], w_up=f(w_up)[0], w_down=f(w_down)[0],
        gmix=np.ascontiguousarray(f(norm_mix)[0].reshape(8, 128).T), gffn=np.ascontiguousarray(f(norm_ffn)[0].reshape(8, 128).T),
        nfin=f(norm_final), hgn=f(hg_norm)[0].reshape(128, 1),
        lbp=np.ascontiguousarray(f(lb_param).reshape(2, 4, 128).transpose(2, 0, 1).reshape(128, 8)),
        a_re=_pair(f(s5_a_re)[0]), a_im=_pair(f(s5_a_im)[0]),
        ldt=_pair(np.ascontiguousarray(np.broadcast_to(f(s5_log_dt)[0][:, None], (32, 64)))),
        b_re=_pair(f(s5_b_re)[0]), b_im=_pair(f(s5_b_im)[0]),
        c_re=_pair(np.ascontiguousarray(f(s5_c_re)[0].transpose(0, 2, 1))), c_im=_pair(np.ascontiguousarray(f(s5_c_im)[0].transpose(0, 2, 1))),
        dsk=np.ascontiguousarray(f(s5_d)[0].reshape(4, 128).T),
        **cst)
    in_maps = []
    for c in range(8):
        xin = np.concatenate([x_prompt[c], x_sample[16 * c:16 * c + 16].reshape(128, 1024)], axis=0)
        sre = np.ascontiguousarray(np.moveaxis(st_re[16 * c:16 * c + 16], 0, -1))
        sim = np.ascontiguousarray(np.moveaxis(st_im[16 * c:16 * c + 16], 0, -1))
        m = dict(shared)
        m.update(xin=np.ascontiguousarray(xin), s5re_in=_pair(sre), s5im_in=_pair(sim), hg_in=np.ascontiguousarray(st_hg[16 * c:16 * c + 16]))
        in_maps.append(m)
    res = run_bass_kernel_spmd(nc, in_maps, core_ids=list(range(8)))
    rs = res.results
    y_prompt = np.stack([rs[c]["y"][:2048] for c in range(8)])
    y_sample = np.concatenate([rs[c]["y"][2048:].reshape(16, 8, 1024) for c in range(8)])
    p_re = np.stack([_unpair(rs[c]["s5p_re"]) for c in range(8)])[None]
    p_im = np.stack([_unpair(rs[c]["s5p_im"]) for c in range(8)])[None]
    p_hg = np.stack([rs[c]["hgp"] for c in range(8)])[None]
    s_re = np.concatenate([np.moveaxis(_unpair(rs[c]["s5s_re"]), -1, 0) for c in range(8)])[None]
    s_im = np.concatenate([np.moveaxis(_unpair(rs[c]["s5s_im"]), -1, 0) for c in range(8)])[None]
    s_hg = np.concatenate([rs[c]["hgs"] for c in range(8)])[None]
    out = (y_prompt, y_sample, p_re, p_im, p_hg, s_re, s_im, s_hg)
    return tuple(np.ascontiguousarray(o, dtype=np.float32) for o in out)
```
